# Optimizing a Trainium2 kernel written in Bass

```python
import jax, jax.numpy as jnp
from jax import lax
import numpy as np

D_MODEL = 1024
BATCH = 16
SEQ = 2048
DEPTH = 1

POOL_WIDTH = D_MODEL // 2
POOL_WINDOWS = (2, 4, 8, 16)
N_POOL_GROUPS = len(POOL_WINDOWS)
POOL_GROUP = POOL_WIDTH // N_POOL_GROUPS
SSD_HEAD_DIM = 64
SSD_INNER = D_MODEL
SSD_HEADS = SSD_INNER // SSD_HEAD_DIM
SSD_GROUPS = 2
SSD_HPG = SSD_HEADS // SSD_GROUPS
SSD_STATE = 128
CONV_WIDTH = 4
CHUNK = 128
CONV_CH = SSD_INNER + 2 * SSD_GROUPS * SSD_STATE
MIX_WIDTH = POOL_WIDTH + SSD_INNER
OFF_POOL = 0
OFF_Z = OFF_POOL + POOL_WIDTH
OFF_XBC = OFF_Z + SSD_INNER
OFF_DT = OFF_XBC + CONV_CH
IN_WIDTH = OFF_DT + SSD_HEADS
D_FF = 4 * D_MODEL
N_MOD = 6
EPS = 1e-5

kernel_name = "hybrid_pool_ssd_adaln_block"


def rms_norm(x, g):
    x32 = x.astype(jnp.float32)
    y = x32 * lax.rsqrt(jnp.mean(x32 * x32, axis=-1, keepdims=True) + EPS)
    return (y * g.astype(jnp.float32)).astype(x.dtype)


def pool_mixer(u, w_pool, pool_scale):
    Bsz, S, _ = u.shape
    u32 = u.astype(jnp.float32)
    cs = jnp.cumsum(u32, axis=1)
    t = jnp.arange(1, S + 1, dtype=jnp.float32)[None, :, None]
    outs = []
    for gi, w in enumerate(POOL_WINDOWS):
        sl = slice(gi * POOL_GROUP, (gi + 1) * POOL_GROUP)
        cs_g = cs[..., sl]
        prev = jnp.pad(cs_g, ((0, 0), (w, 0), (0, 0)))[:, :S]
        mean = (cs_g - prev) / jnp.minimum(t, float(w))
        outs.append(mean - u32[..., sl])
    p = jnp.stack(outs, axis=2).astype(u.dtype)
    y = jnp.einsum('bsgc,gcd->bsgd', p, w_pool).reshape(Bsz, S, POOL_WIDTH)
    return y * pool_scale


def causal_depthwise_conv(u, w, b):
    K = w.shape[0]
    S = u.shape[1]
    up = jnp.pad(u, ((0, 0), (K - 1, 0), (0, 0)))
    out = b + up[:, 0:S] * w[0]
    for k in range(1, K):
        out = out + up[:, k:k + S] * w[k]
    return out


def ssd_chunked(xs, dt, a, bm, cm):
    Bsz, S, G, R, P = xs.shape
    N = bm.shape[-1]
    nc = S // CHUNK
    xdt = (xs.astype(jnp.float32) * dt[..., None]).reshape(Bsz, nc, CHUNK, G, R, P)
    da = (dt * a).reshape(Bsz, nc, CHUNK, G, R)
    bc = bm.astype(jnp.float32).reshape(Bsz, nc, CHUNK, G, N)
    cc = cm.astype(jnp.float32).reshape(Bsz, nc, CHUNK, G, N)
    a_cum = jnp.cumsum(da, axis=2)
    causal = jnp.tril(jnp.ones((CHUNK, CHUNK), dtype=bool))[None, None, :, :, None, None]
    seg = a_cum[:, :, :, None] - a_cum[:, :, None, :]
    decay_in = jnp.exp(jnp.where(causal, seg, -jnp.inf))
    scores = jnp.einsum('bclgn,bcsgn->bclsg', cc, bc)
    y_diag = jnp.einsum('bclsgr,bcsgrp->bclgrp', scores[..., None] * decay_in, xdt)
    decay_out = jnp.exp(a_cum[:, :, -1:] - a_cum)
    states = jnp.einsum('bclgn,bclgr,bclgrp->bcgrpn', bc, decay_out, xdt)
    chunk_decay = jnp.exp(a_cum[:, :, -1])

    def step(h, inp):
        st, dec = inp
        return h * dec[..., None, None] + st, h

    h0 = jnp.zeros((Bsz, G, R, P, N), jnp.float32)
    _, prev = lax.scan(step, h0, (jnp.moveaxis(states, 1, 0), jnp.moveaxis(chunk_decay, 1, 0)))
    prev = jnp.moveaxis(prev, 0, 1)
    y_off = jnp.einsum('bclgn,bcgrpn,bclgr->bclgrp', cc, prev, jnp.exp(a_cum))
    return (y_diag + y_off).reshape(Bsz, S, G, R, P)


def ssd_mixer(z, u_xbc, u_dt, conv_w, conv_b, dt_bias, a_log, d_skip, g_ssd):
    Bsz, S, _ = z.shape
    xbc = jax.nn.silu(causal_depthwise_conv(u_xbc, conv_w, conv_b))
    GN = SSD_GROUPS * SSD_STATE
    xs = xbc[..., :SSD_INNER].reshape(Bsz, S, SSD_GROUPS, SSD_HPG, SSD_HEAD_DIM)
    bm = xbc[..., SSD_INNER:SSD_INNER + GN].reshape(Bsz, S, SSD_GROUPS, SSD_STATE)
    cm = xbc[..., SSD_INNER + GN:].reshape(Bsz, S, SSD_GROUPS, SSD_STATE)
    dt = jax.nn.softplus(u_dt.astype(jnp.float32) + dt_bias.astype(jnp.float32))
    dt = dt.reshape(Bsz, S, SSD_GROUPS, SSD_HPG)
    a = -jnp.exp(a_log.astype(jnp.float32)).reshape(SSD_GROUPS, SSD_HPG)
    y = ssd_chunked(xs, dt, a, bm, cm)
    y = y + d_skip.astype(jnp.float32).reshape(SSD_GROUPS, SSD_HPG)[:, :, None] * xs.astype(jnp.float32)
    y = y.reshape(Bsz, S, SSD_INNER) * jax.nn.silu(z.astype(jnp.float32))
    yg = y.reshape(Bsz, S, SSD_GROUPS, SSD_INNER // SSD_GROUPS)
    yg = yg * lax.rsqrt(jnp.mean(yg * yg, axis=-1, keepdims=True) + EPS)
    y = yg.reshape(Bsz, S, SSD_INNER) * g_ssd.astype(jnp.float32)
    return y.astype(z.dtype)


def setup_inputs(seed: int = 0) -> dict:
    key = jax.random.key(seed)
    ks = jax.random.split(key, 20)
    f32 = jnp.float32
    L = DEPTH
    x = jax.random.normal(ks[0], (BATCH, SEQ, D_MODEL), f32)
    c = jax.random.normal(ks[1], (BATCH, D_MODEL), f32)
    w_ada = jax.random.normal(ks[2], (L, D_MODEL, N_MOD * D_MODEL), f32) * D_MODEL ** -0.5
    b_ada = jax.random.normal(ks[3], (L, N_MOD * D_MODEL), f32) * 0.02
    g_mix = 1.0 + 0.02 * jax.random.normal(ks[4], (L, D_MODEL), f32)
    w_in = jax.random.normal(ks[5], (L, D_MODEL, IN_WIDTH), f32) * D_MODEL ** -0.5
    conv_w = jax.random.normal(ks[6], (L, CONV_WIDTH, CONV_CH), f32) * CONV_WIDTH ** -0.5
    conv_b = jax.random.normal(ks[7], (L, CONV_CH), f32) * 0.02
    dt0 = jnp.exp(jax.random.uniform(ks[8], (L, SSD_HEADS), f32, np.log(1e-3), np.log(1e-1)))
    dt_bias = dt0 + jnp.log(-jnp.expm1(-dt0))
    a_log = jnp.log(jax.random.uniform(ks[9], (L, SSD_HEADS), f32, 1.0, 16.0))
    d_skip = 1.0 + 0.1 * jax.random.normal(ks[10], (L, SSD_HEADS), f32)
    g_ssd = 1.0 + 0.02 * jax.random.normal(ks[11], (L, SSD_INNER), f32)
    w_pool = jax.random.normal(ks[12], (L, N_POOL_GROUPS, POOL_GROUP, POOL_GROUP), f32) * POOL_GROUP ** -0.5
    pool_scale = 1.0 + 0.1 * jax.random.normal(ks[13], (L, POOL_WIDTH), f32)
    w_out = jax.random.normal(ks[14], (L, MIX_WIDTH, D_MODEL), f32) * MIX_WIDTH ** -0.5
    g_mlp = 1.0 + 0.02 * jax.random.normal(ks[15], (L, D_MODEL), f32)
    w_up = jax.random.normal(ks[16], (L, D_MODEL, D_FF), f32) * D_MODEL ** -0.5
    w_down = jax.random.normal(ks[17], (L, D_FF, D_MODEL), f32) * D_FF ** -0.5
    g_final = 1.0 + 0.02 * jax.random.normal(ks[18], (D_MODEL,), f32)
    return {"x": x, "c": c, "w_ada": w_ada, "b_ada": b_ada, "g_mix": g_mix, "w_in": w_in,
            "conv_w": conv_w, "conv_b": conv_b, "dt_bias": dt_bias, "a_log": a_log,
            "d_skip": d_skip, "g_ssd": g_ssd, "w_pool": w_pool, "pool_scale": pool_scale,
            "w_out": w_out, "g_mlp": g_mlp, "w_up": w_up, "w_down": w_down, "g_final": g_final}


def reference(x, c, w_ada, b_ada, g_mix, w_in, conv_w, conv_b, dt_bias, a_log, d_skip, g_ssd,
              w_pool, pool_scale, w_out, g_mlp, w_up, w_down, g_final):
    h = x
    c_act = jax.nn.silu(c)
    for layer in range(DEPTH):
        mod = jnp.einsum('bd,de->be', c_act, w_ada[layer]) + b_ada[layer]
        shift_m, scale_m, gate_m, shift_f, scale_f, gate_f = jnp.split(mod[:, None, :], N_MOD, axis=-1)

        u = rms_norm(h, g_mix[layer]) * (1.0 + scale_m) + shift_m
        proj = jnp.einsum('bsd,de->bse', u, w_in[layer])
        u_pool = proj[..., OFF_POOL:OFF_Z]
        z = proj[..., OFF_Z:OFF_XBC]
        u_xbc = proj[..., OFF_XBC:OFF_DT]
        u_dt = proj[..., OFF_DT:]
        y_pool = pool_mixer(u_pool, w_pool[layer], pool_scale[layer])
        y_ssd = ssd_mixer(z, u_xbc, u_dt, conv_w[layer], conv_b[layer], dt_bias[layer],
                          a_log[layer], d_skip[layer], g_ssd[layer])
        y_mix = jnp.concatenate([y_pool.astype(h.dtype), y_ssd.astype(h.dtype)], axis=-1)
        h = h + gate_m * jnp.einsum('bse,ed->bsd', y_mix, w_out[layer])

        u = rms_norm(h, g_mlp[layer]) * (1.0 + scale_f) + shift_f
        f = jnp.square(jax.nn.relu(jnp.einsum('bsd,df->bsf', u, w_up[layer])))
        h = h + gate_f * jnp.einsum('bsf,fd->bsd', f, w_down[layer])
    return rms_norm(h, g_final)
```

```python
import types
import numpy as np
from contextlib import ExitStack
import concourse.bass as bass
import concourse.mybir as mybir
from concourse.bass_utils import run_bass_kernel_spmd

F32 = mybir.dt.float32
BF16 = mybir.dt.bfloat16
AF = mybir.ActivationFunctionType
ALU = mybir.AluOpType

NCORES = 8
D = 1024
SEQ = 2048
NB = 2
TT = 256
NTS = SEQ // TT
NT = NB * NTS
EPS = 1e-5
IN_W = 3088
OFF_Z = 512
OFF_XBC = 1536
OFF_DT = 3072
POOL_W = (2, 4, 8, 16)

V_GMIX, V_GMLP, V_CW, V_CB, V_PSC, V_GSSD, V_BT, V_DTB, V_ALOG, V_DSK, V_INVC, V_CT = (
    0, 8, 16, 64, 76, 80, 88, 136, 152, 168, 184, 248)
NV = 264


def _freeze(fn):
    if fn.__closure__ is None:
        return fn
    cells = tuple(types.CellType(c.cell_contents) for c in fn.__closure__)
    return types.FunctionType(fn.__code__, fn.__globals__, fn.__name__, fn.__defaults__, cells)


def _cost(e, n):
    if e == 'pe':
        return 6.0 + 0.45 * n + (0.125 * (n - 256) if n > 256 else 0.0)
    if e == 'act':
        return 40.0 + (250.0 + n) / 1.5
    if e == 'dve':
        return (200.0 + n) / 0.9
    if e == 'pool':
        return 200.0 + 1.8 * n
    return 60.0


class Sched:
    ENG = ('pe', 'act', 'dve', 'pool', 'sp')
    DMA_BW = 280.0
    DMA_LAT = 2000.0
    XLAT = 600.0
    SLAT = 120.0

    def __init__(self, nc, es):
        self.nc = nc
        self.es = es
        self.eng = {'pe': nc.tensor, 'act': nc.scalar, 'dve': nc.vector,
                    'pool': nc.gpsimd, 'sp': nc.sync}
        self.sem = {e: es.enter_context(nc.semaphore("c_" + e)) for e in self.ENG}
        self.cnt = {e: 0 for e in self.ENG}
        self.seen = {e: {} for e in self.ENG}
        self.dma_sems = {}
        self.res = {}
        self.ops = []
        self.sim_total = 0.0
        self.sz = None

    def _r(self, name):
        st = self.res.get(name)
        if st is None:
            st = {'w': None, 'r': []}
            self.res[name] = st
        return st

    def _record(self, rec, reads, writes):
        ops = self.ops
        e = rec['e']
        is_dma = rec['dma']
        deps = {}

        def add(d, raw):
            o = ops[d]
            need = is_dma or o['dma'] or o['e'] != e or e != 'pe'
            deps[d] = deps.get(d, False) or need

        for r in reads:
            w = self._r(r)['w']
            if w is not None:
                add(w, True)
        for wn in writes:
            st = self._r(wn)
            if st['w'] is not None:
                add(st['w'], False)
            for d in st['r']:
                add(d, False)
        rec['deps'] = deps
        rec['id'] = len(ops)
        ops.append(rec)
        for r in reads:
            self._r(r)['r'].append(rec['id'])
        for wn in writes:
            st = self._r(wn)
            st['w'] = rec['id']
            st['r'] = []

    def op(self, e, fn, reads=(), writes=(), n=None):
        if n is None:
            n = self.sz if self.sz is not None else 64
        self.sz = None
        self._record({'e': e, 'dma': False, 'fn': _freeze(fn), 'cost': _cost(e, n)}, reads, writes)

    def dma(self, q, out, in_, reads=(), writes=(), sem=None, nbytes=None):
        if nbytes is None:
            nbytes = self.sz if self.sz is not None else 65536
        self.sz = None
        name = sem or ('d_' + (writes[0] if writes else reads[0]))
        if name not in self.dma_sems:
            self.dma_sems[name] = [self.es.enter_context(self.nc.semaphore(name)), 0]
        self._record({'e': q, 'dma': True, 'out': out, 'in_': in_, 'sem': name, 'cost': 80.0,
                      'nbytes': nbytes}, reads, writes)

    INORDER_SEGS = (0,)
    seg_idx = 0

    def _schedule(self):
        ops = self.ops
        nops = len(ops)
        seg = self.seg_idx
        self.seg_idx += 1
        if seg in self.INORDER_SEGS:
            order = {e: [] for e in self.ENG}
            for o in ops:
                order[o['e']].append(o['id'])
            return order
        succ = [[] for _ in range(nops)]
        nun = [0] * nops
        for o in ops:
            nun[o['id']] = len(o['deps'])
            for d in o['deps']:
                succ[d].append(o['id'])
        ready = {e: [] for e in self.ENG}
        rtime = [0.0] * nops
        fin = [0.0] * nops
        for o in ops:
            if nun[o['id']] == 0:
                ready[o['e']].append(o['id'])
        efree = {e: 0.0 for e in self.ENG}
        dma_free = 0.0
        order = {e: [] for e in self.ENG}
        done = 0
        while done < nops:
            best = None
            for e in self.ENG:
                rl = ready[e]
                if not rl:
                    continue
                tmin = min(rtime[i] for i in rl)
                t = max(efree[e], tmin)
                if best is None or t < best[0]:
                    best = (t, e)
            t, e = best
            rl = ready[e]
            cand = min(i for i in rl if rtime[i] <= t)
            rl.remove(cand)
            o = ops[cand]
            start = t
            efree[e] = start + o['cost']
            if o['dma']:
                st = max(efree[e], dma_free)
                dma_free = st + o['nbytes'] / self.DMA_BW
                fin[cand] = dma_free + self.DMA_LAT
            else:
                fin[cand] = efree[e]
            order[e].append(cand)
            done += 1
            for s_ in succ[cand]:
                nun[s_] -= 1
                lat = fin[cand] + (self.XLAT if (ops[s_]['e'] != e or o['dma']) else self.SLAT)
                if lat > rtime[s_]:
                    rtime[s_] = lat
                if nun[s_] == 0:
                    ready[ops[s_]['e']].append(s_)
        self.sim_total += max(fin) if nops else 0.0
        return order

    def flush(self):
        ops = self.ops
        if not ops:
            return
        order = self._schedule()
        pos = [0] * len(ops)
        for e in self.ENG:
            for p_, i in enumerate(order[e]):
                pos[i] = p_
        waited = [False] * len(ops)
        sel = [None] * len(ops)
        for o in ops:
            best = {}
            lst = []
            for d, w in o['deps'].items():
                if not w:
                    continue
                od = ops[d]
                if od['dma']:
                    lst.append(d)
                else:
                    k = od['e']
                    if k not in best or pos[best[k]] < pos[d]:
                        best[k] = d
            lst.extend(best.values())
            sel[o['id']] = lst
            for d in lst:
                waited[d] = True
        for e in self.ENG:
            nd = [i for i in order[e] if not ops[i]['dma']]
            if nd:
                waited[nd[-1]] = True
        tok = [None] * len(ops)
        for e in self.ENG:
            c = self.cnt[e]
            dc = {}
            for i in order[e]:
                o = ops[i]
                if o['dma']:
                    rec = self.dma_sems[o['sem']]
                    v = dc.get(o['sem'], rec[1]) + 16
                    dc[o['sem']] = v
                    tok[i] = (o['sem'], v)
                elif waited[i]:
                    c += 1
                    tok[i] = (e, c)
        for e in self.ENG:
            eng = self.eng[e]
            seen = self.seen[e]
            for i in order[e]:
                o = ops[i]
                need = {}
                for d in sel[i]:
                    k, v = tok[d]
                    if need.get(k, 0) < v:
                        need[k] = v
                for k, v in need.items():
                    if seen.get(k, 0) < v:
                        eng.wait_ge(self.sem[k] if k in self.sem else self.dma_sems[k][0], v)
                        seen[k] = v
                if o['dma']:
                    ins = eng.dma_start(out=o['out'], in_=o['in_'])
                    rec = self.dma_sems[o['sem']]
                    rec[1] += 16
                    assert rec[1] == tok[i][1]
                    ins.then_inc(rec[0], 16)
                else:
                    ins = o['fn'](eng)
                    if waited[i]:
                        self.cnt[e] += 1
                        assert self.cnt[e] == tok[i][1]
                        ins.then_inc(self.sem[e], 1)
        self.ops = []
        self.res = {}

    def wait_all(self, e):
        for k in self.ENG:
            if k != e and self.cnt[k] > self.seen[e].get(k, 0):
                self.eng[e].wait_ge(self.sem[k], self.cnt[k])
                self.seen[e][k] = self.cnt[k]
        for k, (s, v) in self.dma_sems.items():
            if v > self.seen[e].get(k, 0):
                self.eng[e].wait_ge(s, v)
                self.seen[e][k] = v

    def barrier(self):
        self.flush()
        for e in self.ENG:
            self.wait_all(e)


def build_nc(debug=False):
    nc = bass.Bass("TRN2", target_bir_lowering=False)

    def din(name, shape):
        return nc.dram_tensor(name, shape, F32, kind="ExternalInput").ap()

    x_d = din("x", [NB, SEQ, D])
    vecs_d = din("vecs", [128, NV])
    consts_d = din("consts", [128, 512])
    gbias_d = din("gbias", [128, 2048])
    gfin_d = din("gfin", [128, D])
    rowv_d = din("rowv", [128, 48])
    wada_d = din("w_ada", [D, 6 * D])
    win_d = din("w_in", [D, IN_W])
    wout_d = din("w_out", [1536, D])
    wup_d = din("w_up", [D, 4 * D])
    wdown_d = din("w_down", [4 * D, D])
    wpool_d = din("w_pool", [512, 128])
    out_d = nc.dram_tensor("out", [NB, SEQ, D], F32, kind="ExternalOutput").ap()
    wupbf_d = nc.dram_tensor("wup_bf", [128, 4, 8, D], BF16, kind="Internal").ap()
    wdnbf_d = nc.dram_tensor("wdn_bf", [128, 32, D], BF16, kind="Internal").ap()
    h1s_d = nc.dram_tensor("h1s", [NB * SEQ, D], F32,
                           kind="ExternalOutput" if debug else "Internal").ap()

    with ExitStack() as es0:
        S = Sched(nc, es0)

        def sbt(es, name, shape, dt=F32):
            return es.enter_context(nc.sbuf_tensor("s_" + name, shape, dt))

        psb = [es0.enter_context(nc.psum_tensor("psb%d" % i, [128, 512], F32)) for i in range(8)]
        ps_free = [True] * 8
        ps_next = [0]

        ps_ring = [None]
        ps_nx = {}

        def ps_alloc(ring=None):
            lo, hi = ring or ps_ring[0] or (0, 8)
            n_ = hi - lo
            st = ps_nx.get((lo, hi), 0)
            for t in range(n_):
                i = lo + (st + t) % n_
                if ps_free[i]:
                    ps_free[i] = False
                    ps_nx[(lo, hi)] = (i - lo + 1) % n_
                    return i
            raise RuntimeError("no free PSUM bank")

        def ps_rel(i):
            ps_free[i] = True

        def pv(i):
            return psb[i][:]

        def pvb(i):
            return psb[i][:].bitcast(BF16)

        def pn(i):
            return "ps%d" % i

        vecs = sbt(es0, "vecs", [128, NV])
        rowv = sbt(es0, "rowv", [128, 48])
        cst = sbt(es0, "cst", [128, 512], BF16)
        ident = cst[:, 0:128]
        tri = cst[:, 128:256]
        SLm = cst[:, 256:384]
        ones = cst[:, 384:512]
        cact = sbt(es0, "cact", [128, 16], BF16)
        G1 = sbt(es0, "G1", [128, 8, 2])
        S1 = sbt(es0, "S1", [128, 8, 2])
        G2 = sbt(es0, "G2", [128, 8, 2])
        S2 = sbt(es0, "S2", [128, 8, 2])
        gm32 = sbt(es0, "gm32", [128, 16])
        a_bc = sbt(es0, "a_bc", [128, 16])
        mhalf = sbt(es0, "mhalf", [128, 4])
        smallt = sbt(es0, "smallt", [128, 16])

        S.dma('sp', vecs[:], vecs_d, writes=['vecs'])
        S.dma('sp', rowv[:], rowv_d, writes=['rowv'])
        S.sz = 262144
        S.dma('pool', cst[:], consts_d, writes=['cst'])
        S.op('pool', lambda e: e.memset(mhalf[:], -0.5), writes=['mhalf'])

        def mm(out, lhsT, rhs, start, stop, R, W, **kw):
            S.sz = int(np.prod(rhs.shape[1:]))
            S.op('pe', lambda e: e.matmul(out, lhsT, rhs, start=start, stop=stop, **kw),
                 reads=R, writes=W)

        def tp(out, in_, R, W):
            S.sz = 128
            S.op('pe', lambda e: e.transpose(out, in_, ident), reads=R + ['cst'], writes=W)

        S.op('act', lambda e: e.activation(out=cact[:], in_=vecs[:, V_CT:V_CT + 16], func=AF.Silu),
             reads=['vecs'], writes=['cact'])
        S.op('act', lambda e: e.activation(out=a_bc[:], in_=rowv[:, 16:32], func=AF.Exp),
             reads=['rowv'], writes=['a_bc'])
        S.op('dve', lambda e: e.tensor_scalar(a_bc[:], a_bc[:], -1.0, None, op0=ALU.mult),
             reads=['a_bc'], writes=['a_bc'])
        S.op('dve', lambda e: e.tensor_scalar(gm32[:], vecs[:, V_GMIX:V_GMIX + 16], 32.0, None, op0=ALU.mult),
             reads=['vecs'], writes=['gm32'])

        def bc3(ap2, n):
            return ap2.unsqueeze(2).broadcast_to([ap2.shape[0], ap2.shape[1], n])

        def ada_block(blk, wbuf, wname, crep, gate=None):
            S.sz = 4194304
            S.dma('pool', wbuf[:], wada_d[:, blk * 1024:(blk + 1) * 1024].rearrange("(k p) c -> p k c", p=128),
                  writes=[wname])
            if gate is None:
                b = ps_alloc()
                pm = pv(b)[:, 0:16].rearrange("p (e j) -> p e j", j=2)
                for e_ in range(8):
                    for k in range(8):
                        mm(pm[:, e_, :], wbuf[:, k, e_ * 128:(e_ + 1) * 128], cact[:, 2 * k:2 * k + 2],
                           k == 0, k == 7, [wname, 'cact'], [pn(b)])
                bt = bc3(vecs[:, V_BT + blk * 8:V_BT + blk * 8 + 8], 2)
                dst, dn = {0: (S1, 'S1'), 1: (G1, 'G1'), 3: (S2, 'S2'), 4: (G2, 'G2')}[blk]
                S.sz = 16
                S.op('dve', lambda e: e.tensor_tensor(dst[:], pm, bt, op=ALU.add),
                     reads=[pn(b), 'vecs'], writes=[dn])
                ps_rel(b)
                if blk in (1, 4):
                    gsel = gm32[:, 0:8] if blk == 1 else gm32[:, 8:16]
                    S.sz = 16
                    S.op('dve', lambda e: e.scalar_tensor_tensor(out=dst[:], in0=dst[:], scalar=1.0,
                                                                 in1=bc3(gsel, 2), op0=ALU.add, op1=ALU.mult),
                         reads=[dn, 'gm32'], writes=[dn])
            else:
                gbuf, gname = gate
                for j in range(2):
                    for dh in range(2):
                        b = ps_alloc()
                        for k in range(8):
                            mm(pv(b), crep[:, 2 * k + j, :], wbuf[:, k, dh * 512:(dh + 1) * 512],
                               k == 0, k == 7, [wname, 'crep'], [pn(b)])
                        gs = gbuf[:, j, dh * 512:(dh + 1) * 512]
                        S.sz = 512
                        S.op('dve', lambda e, gs=gs, b=b: e.tensor_tensor(gs, pv(b), gs, op=ALU.add),
                             reads=[pn(b), gname], writes=[gname])
                        ps_rel(b)

        def make_crep(crep):
            S.sz = 2048
            S.op('dve', lambda e: e.tensor_copy(crep[:], bc3(cact[:], 128)), reads=['cact'], writes=['crep'])

        with ExitStack() as esM:
            w_in = sbt(esM, "w_in", [128, 8, IN_W], BF16)
            w_out = sbt(esM, "w_out", [128, 12, D], BF16)
            w_pool = sbt(esM, "w_pool", [128, 4, 128], BF16)
            gate_m = sbt(esM, "gate_m", [128, 2, D])
            diagC = sbt(esM, "diagC", [128, 48, 128], BF16)
            diagD = sbt(esM, "diagD", [128, 8, 128], BF16)
            S.sz = 6144
            S.op('dve', lambda e: e.tensor_tensor(diagC[:], ident.unsqueeze(1).broadcast_to([128, 48, 128]),
                                                  bc3(vecs[:, V_CW:V_CW + 48], 128), op=ALU.mult),
                 reads=['cst', 'vecs'], writes=['diagC'])
            S.sz = 1024
            S.op('dve', lambda e: e.tensor_tensor(diagD[:], ident.unsqueeze(1).broadcast_to([128, 8, 128]),
                                                  bc3(vecs[:, V_DSK:V_DSK + 8], 128), op=ALU.mult),
                 reads=['cst', 'vecs'], writes=['diagD'])

            with ExitStack() as esP:
                wada = [sbt(esP, "wada%d" % i, [128, 8, 1024], BF16) for i in range(2)]
                crep = sbt(esP, "crep", [128, 16, 128], BF16)
                make_crep(crep)
                for j in range(2):
                    S.sz = 524288
                    S.dma('sp', gate_m[:, j, :], gbias_d[:, 0:1024], writes=['gate_m'], sem='d_gm%d' % j)
                ada_block(0, wada[0], 'wada0', crep)
                ada_block(1, wada[1], 'wada1', crep)
                for k in range(8):
                    S.sz = 1581056
                    S.dma('pool', w_in[:, k, :], win_d[k * 128:(k + 1) * 128, :], writes=['w_in%d' % k])
                S.sz = 262144
                S.dma('pool', w_pool[:], wpool_d.rearrange("(g p) d -> p g d", p=128), writes=['w_pool'])
                ada_block(2, wada[0], 'wada0', crep, gate=(gate_m, 'gate_m'))
                for i in range(3):
                    S.sz = 2097152
                    S.dma('pool', w_out[:, 4 * i:4 * i + 4, :],
                          wout_d[i * 512:(i + 1) * 512, :].rearrange("(e p) d -> p e d", p=128),
                          writes=['w_out%d' % i])
                ada_block(3, wada[1], 'wada1', crep)
                ada_block(4, wada[0], 'wada0', crep)
                S.barrier()
            WIN = ['w_in%d' % k for k in range(8)]
            WOUT = ['w_out%d' % i for i in range(3)]

            xt = [sbt(esM, "xt%d" % i, [128, D]) for i in range(2)]
            xn = [sbt(esM, "xn%d" % i, [128, D], BF16) for i in range(2)]
            ssn = sbt(esM, "ssn", [128, 8])
            uTb = [sbt(esM, "uT%d" % i, [128, 8, TT], BF16) for i in range(2)]
            up = [sbt(esM, "up%d" % i, [128, 2, 16 + TT]) for i in range(2)]
            ta = [sbt(esM, "ta%d" % i, [128, 16 + TT]) for i in range(2)]
            tb = [sbt(esM, "tb%d" % i, [128, 16 + TT]) for i in range(2)]
            t16 = sbt(esM, "t16", [128, 4, 16])
            pbf = sbt(esM, "pbf", [128, 4, TT], BF16)
            pool_halo = sbt(esM, "pool_halo", [128, 4, 16])
            pre = [sbt(esM, "pre%d" % i, [128, 2, 4 + TT], BF16) for i in range(3)]
            conv_halo = sbt(esM, "conv_halo", [128, 12, 3], BF16)
            xbcT = [sbt(esM, "xbcT%d" % i, [128, 12, TT], BF16) for i in range(2)]
            sz = [sbt(esM, "sz%d" % i, [128, 2, D], BF16) for i in range(1)] * 2
            dtb = sbt(esM, "dtb", [128, 2, 16])
            dtv = [sbt(esM, "dtv%d" % i, [128, 2, 16]) for i in range(2)]
            dabf = [sbt(esM, "dabf%d" % i, [128, 2, 16], BF16) for i in range(2)]
            Rr = [sbt(esM, "Rr%d" % i, [128, 4, 128], BF16) for i in range(4)]
            Eq = [sbt(esM, "Eq%d" % i, [128, 4, 128], BF16) for i in range(3)]
            MT = [sbt(esM, "MT%d" % i, [128, 16, 128], BF16) for i in range(2)]
            scm = [sbt(esM, "scm%d" % i, [128, 2, 128], BF16) for i in range(2)]
            sm = [sbt(esM, "sm%d" % i, [128, 48]) for i in range(2)]
            xdt = [sbt(esM, "xdt%d" % i, [128, D], BF16) for i in range(2)]
            xdtd = [sbt(esM, "xdtd%d" % i, [128, D], BF16) for i in range(2)]
            Btok = [sbt(esM, "Btok%d" % i, [128, 256], BF16) for i in range(2)]
            Hs = sbt(esM, "Hs", [128, D])
            Hbf = sbt(esM, "Hbf", [128, D], BF16)
            ybuf = sbt(esM, "ybuf", [128, D])
            yn = sbt(esM, "yn", [128, D], BF16)
            ssy = sbt(esM, "ssy", [128, 4])
            ypoolT = [sbt(esM, "ypoolT%d" % i, [128, 4, TT], BF16) for i in range(2)]
            yssdT = sbt(esM, "yssdT", [128, 8, TT], BF16)
            tbuf = [sbt(esM, "tbuf%d" % i, [128, 512]) for i in range(2)]
            x2 = [sbt(esM, "x2_%d" % i, [128, D]) for i in range(2)]
            cnt = {'xt': 0, 'pre': 0, 'acc': 0, 'up': 0, 'R': 0, 'E': 0, 'x2': 0}

            def nxt(key, n):
                v = cnt[key] % n
                cnt[key] += 1
                return v

            prepT = {}

            def prep_a(n):
                j, T = divmod(n, NTS)
                banks = [ps_alloc(), ps_alloc()]
                prepT[n] = banks
                for s in range(2):
                    sl = nxt('xt', 2)
                    tok0 = T * TT + s * 128
                    S.sz = 524288
                    S.dma('sp', xt[sl][:], x_d[j, tok0:tok0 + 128, :], writes=['xt%d' % sl])
                    S.op('pool', lambda e, sl=sl: e.memset(ssn[:, 4 * sl:4 * sl + 1], 0.0), writes=['ssn%d' % sl])
                    S.sz = 1024
                    S.op('act', lambda e, sl=sl: e.activation(out=xn[sl][:], in_=xt[sl][:], func=AF.Square,
                                                              accum_out=ssn[:, 4 * sl:4 * sl + 1]),
                         reads=['xt%d' % sl, 'ssn%d' % sl], writes=['xn%d' % sl, 'ssn%d' % sl])
                    S.op('pool', lambda e, sl=sl: e.tensor_scalar(ssn[:, 4 * sl + 1:4 * sl + 2], ssn[:, 4 * sl:4 * sl + 1],
                                                                  1024.0 * EPS, None, op0=ALU.add),
                         reads=['ssn%d' % sl], writes=['ssn%d' % sl])
                    S.op('pool', lambda e, sl=sl: e.tensor_tensor(ssn[:, 4 * sl + 2:4 * sl + 3], ssn[:, 4 * sl + 1:4 * sl + 2],
                                                                  mhalf[:, 0:1], op=ALU.pow),
                         reads=['ssn%d' % sl, 'mhalf'], writes=['ssn%d' % sl])
                    S.sz = 1024
                    S.op('act', lambda e, sl=sl: e.activation(out=xn[sl][:], in_=xt[sl][:], func=AF.Copy,
                                                              scale=ssn[:, 4 * sl + 2:4 * sl + 3]),
                         reads=['xt%d' % sl, 'ssn%d' % sl], writes=['xn%d' % sl])
                    for k in range(8):
                        b = banks[k // 4]
                        dst = pvb(b).rearrange("p (k t) -> p k t", k=4)[:, k % 4, s * 128:(s + 1) * 128]
                        tp(dst, xn[sl][:, k * 128:(k + 1) * 128], ['xn%d' % sl], [pn(b)])

            def prep_b(n):
                j, T = divmod(n, NTS)
                banks = prepT.pop(n)
                for k in range(8):
                    b = banks[k // 4]
                    src = pvb(b).rearrange("p (k t) -> p k t", k=4)[:, k % 4, :]
                    S.sz = 256
                    S.op('act', lambda e, k=k, src=src: e.activation(out=uTb[n % 2][:, k, :], in_=src, func=AF.Identity,
                                                                     scale=G1[:, k, j:j + 1], bias=S1[:, k, j:j + 1]),
                         reads=[pn(b), 'G1', 'S1'], writes=['uT%d' % (n % 2)])
                ps_rel(banks[0])
                ps_rel(banks[1])

            def afm(n):
                j, T = divmod(n, NTS)
                par = n % 2
                if T == 0:
                    S.op('pool', lambda e: e.memset(conv_halo[:], 0.0), writes=['conv_halo'])
                    S.op('pool', lambda e: e.memset(pool_halo[:], 0.0), writes=['pool_halo'])
                for pair in range(2):
                    b = ps_alloc()
                    bv = pv(b).rearrange("p (a t) -> p a t", a=2)
                    for half in range(2):
                        g = 2 * pair + half
                        for k in range(8):
                            mm(bv[:, half, :], w_in[:, k, g * 128:(g + 1) * 128], uTb[n % 2][:, k, :], k == 0, k == 7,
                               [WIN[k], 'uT%d' % (n % 2)], [pn(b)])
                    sl = nxt('up', 2)
                    un = 'up%d' % sl
                    S.op('pool', lambda e, sl=sl, pair=pair: e.tensor_copy(up[sl][:, :, 0:16], pool_halo[:, 2 * pair:2 * pair + 2, :]),
                         reads=['pool_halo'], writes=[un])
                    S.sz = 512
                    S.op('dve', lambda e, sl=sl, bv=bv: e.tensor_copy(up[sl][:, :, 16:16 + TT], bv),
                         reads=[pn(b)], writes=[un])
                    ps_rel(b)
                    for half in range(2):
                        g = 2 * pair + half
                        eng = 'dve'
                        A, Bt = ta[half], tb[half]
                        An, Bn = 'ta%d' % half, 'tb%d' % half
                        u = up[sl][:, half, :]
                        L = 16 + TT
                        S.sz = 271
                        S.op(eng, lambda e, u=u, A=A: e.tensor_tensor(A[:, 1:L], u[:, 1:L], u[:, 0:L - 1], op=ALU.add),
                             reads=[un], writes=[An])
                        cur, curn = A, An
                        if g >= 1:
                            S.sz = 271
                            S.op(eng, lambda e, A=A, Bt=Bt: e.tensor_tensor(Bt[:, 3:L], A[:, 3:L], A[:, 1:L - 2], op=ALU.add),
                                 reads=[An], writes=[Bn])
                            cur, curn = Bt, Bn
                        if g >= 2:
                            S.sz = 271
                            S.op(eng, lambda e, A=A, Bt=Bt: e.tensor_tensor(A[:, 7:L], Bt[:, 7:L], Bt[:, 3:L - 4], op=ALU.add),
                                 reads=[Bn], writes=[An])
                            cur, curn = A, An
                        if g >= 3:
                            S.sz = 271
                            S.op(eng, lambda e, A=A, Bt=Bt: e.tensor_tensor(Bt[:, 15:L], A[:, 15:L], A[:, 7:L - 8], op=ALU.add),
                                 reads=[An], writes=[Bn])
                            cur, curn = Bt, Bn
                        w = POOL_W[g]
                        S.sz = 256
                        S.op('dve', lambda e, cur=cur, u=u, g=g, w=w: e.scalar_tensor_tensor(
                            out=pbf[:, g, :], in0=cur[:, 16:L], scalar=1.0 / w, in1=u[:, 16:L],
                            op0=ALU.mult, op1=ALU.subtract), reads=[curn, un], writes=['pbf%d' % g])
                        if T == 0:
                            S.op('dve', lambda e, cur=cur, g=g: e.tensor_tensor(
                                t16[:, g, :], cur[:, 16:32], vecs[:, V_INVC + g * 16:V_INVC + g * 16 + 16], op=ALU.mult),
                                 reads=[curn, 'vecs'], writes=['t16_%d' % g])
                            S.op('dve', lambda e, u=u, g=g: e.tensor_tensor(
                                pbf[:, g, 0:16], t16[:, g, :], u[:, 16:32], op=ALU.subtract),
                                 reads=['t16_%d' % g, un, 'pbf%d' % g], writes=['pbf%d' % g])
                    S.op('pool', lambda e, sl=sl, pair=pair: e.tensor_copy(pool_halo[:, 2 * pair:2 * pair + 2, :], up[sl][:, :, TT:TT + 16]),
                         reads=[un], writes=['pool_halo'])
                for pair in range(2):
                    b = ps_alloc()
                    bv = pv(b).rearrange("p (a t) -> p a t", a=2)
                    for half in range(2):
                        g = 2 * pair + half
                        mm(bv[:, half, :], w_pool[:, g, :], pbf[:, g, :], True, True, ['w_pool', 'pbf%d' % g], [pn(b)])
                    for half in range(2):
                        g = 2 * pair + half
                        S.sz = 256
                        S.op('act', lambda e, g=g, half=half, bv=bv: e.activation(
                            out=ypoolT[par][:, g, :], in_=bv[:, half, :], func=AF.Identity,
                            scale=vecs[:, V_PSC + g:V_PSC + g + 1]), reads=[pn(b), 'vecs'], writes=['ypoolT%d' % par])
                    ps_rel(b)

                for pair in range(6):
                    b = ps_alloc()
                    bv = pv(b).rearrange("p (a t) -> p a t", a=2)
                    for half in range(2):
                        e_ = 2 * pair + half
                        c0 = OFF_XBC + e_ * 128
                        for k in range(8):
                            mm(bv[:, half, :], w_in[:, k, c0:c0 + 128], uTb[n % 2][:, k, :], k == 0, k == 7,
                               [WIN[k], 'uT%d' % (n % 2)], [pn(b)])
                    sl = nxt('pre', 3)
                    prn = 'pre%d' % sl
                    S.op('pool', lambda e, sl=sl, pair=pair: e.tensor_copy(pre[sl][:, :, 0:3], conv_halo[:, 2 * pair:2 * pair + 2, :]),
                         reads=['conv_halo'], writes=[prn])
                    S.sz = 512
                    S.op('dve', lambda e, sl=sl, bv=bv: e.tensor_copy(pre[sl][:, :, 3:3 + TT], bv),
                         reads=[pn(b)], writes=[prn])
                    ps_rel(b)
                    b2 = ps_alloc()
                    bv2 = pv(b2).rearrange("p (a t) -> p a t", a=2)
                    for half in range(2):
                        e_ = 2 * pair + half
                        for kk in range(4):
                            mm(bv2[:, half, :], diagC[:, e_ * 4 + kk, :], pre[sl][:, half, kk:kk + TT], kk == 0, kk == 3,
                               ['diagC', prn], [pn(b2)])
                    for half in range(2):
                        e_ = 2 * pair + half
                        S.sz = 256
                        S.op('act', lambda e, half=half, e_=e_, bv2=bv2: e.activation(
                            out=xbcT[par][:, e_, :], in_=bv2[:, half, :], func=AF.Silu, bias=vecs[:, V_CB + e_:V_CB + e_ + 1]),
                             reads=[pn(b2), 'vecs'], writes=['xbcT%d_%d' % (par, e_)])
                    ps_rel(b2)
                    S.op('pool', lambda e, sl=sl, pair=pair: e.tensor_copy(conv_halo[:, 2 * pair:2 * pair + 2, :], pre[sl][:, :, TT:TT + 3]),
                         reads=[prn], writes=['conv_halo'])
            def atm(n):
                par = n % 2
                bdt = ps_alloc()
                dv = pv(bdt)[:, 0:32].rearrange("p (s h) -> p s h", s=2)
                for s in range(2):
                    bz = [ps_alloc(), ps_alloc()]
                    for k in range(8):
                        lt = uTb[n % 2][:, k, s * 128:(s + 1) * 128]
                        for zh in range(2):
                            mm(pv(bz[zh]), lt, w_in[:, k, OFF_Z + zh * 512:OFF_Z + (zh + 1) * 512], k == 0, k == 7,
                               [WIN[k], 'uT%d' % (n % 2)], [pn(bz[zh])])
                        mm(dv[:, s, :], lt, w_in[:, k, OFF_DT:OFF_DT + 16], k == 0, k == 7, [WIN[k], 'uT%d' % (n % 2)], [pn(bdt)])
                    for zh in range(2):
                        S.sz = 512
                        S.op('act', lambda e, s=s, zh=zh, bz=bz: e.activation(
                            out=sz[par][:, s, zh * 512:(zh + 1) * 512], in_=pv(bz[zh]), func=AF.Silu),
                             reads=[pn(bz[zh])], writes=['sz%d' % s])
                        ps_rel(bz[zh])
                S.op('dve', lambda e: e.tensor_tensor(dtb[:], dv, rowv[:, 0:16].unsqueeze(1).broadcast_to([128, 2, 16]), op=ALU.add),
                     reads=[pn(bdt), 'rowv'], writes=['dtb'])
                ps_rel(bdt)
                S.op('act', lambda e: e.activation(out=dtb[:], in_=dtb[:], func=AF.Exp), reads=['dtb'], writes=['dtb'])
                S.op('act', lambda e: e.activation(out=dtv[par][:], in_=dtb[:], func=AF.Ln, bias=1.0),
                     reads=['dtb'], writes=['dtv%d' % par])
                S.op('dve', lambda e: e.tensor_tensor(dabf[par][:], dtv[par][:], a_bc[:].unsqueeze(1).broadcast_to([128, 2, 16]), op=ALU.mult),
                     reads=['dtv%d' % par, 'a_bc'], writes=['dabf%d' % par])

            def bprep(n, c):
                par = n % 2
                cp = c
                tk = slice(c * 128, (c + 1) * 128)
                XB = ['xbcT%d_%d' % (par, e_) for e_ in range(12)]
                b = ps_alloc()
                rhs = dabf[par][:, c, :]
                for i, lt in enumerate((SLm, ones, tri)):
                    mm(pv(b)[:, 16 * i:16 * i + 16], lt, rhs, True, True, ['cst', 'dabf%d' % par], [pn(b)])
                S.op('act', lambda e, b=b: e.activation(out=sm[cp][:], in_=pv(b)[:, 0:48], func=AF.Exp),
                     reads=[pn(b)], writes=['sm%d' % cp])
                ps_rel(b)
                b = ps_alloc()
                sv = pv(b)[:, 0:256].rearrange("p (g l) -> p g l", g=2)
                for g in range(2):
                    mm(sv[:, g, :], xbcT[par][:, 8 + g, tk], xbcT[par][:, 10 + g, tk], True, True,
                       [XB[8 + g], XB[10 + g]], [pn(b)])
                S.sz = 256
                S.op('dve', lambda e, sv=sv: e.tensor_tensor(scm[cp][:], sv, tri.unsqueeze(1).broadcast_to([128, 2, 128]), op=ALU.mult),
                     reads=[pn(b), 'cst'], writes=['scm%d' % cp])
                ps_rel(b)
                for q in range(4):
                    rs = nxt('R', 4)
                    S.sz = 512
                    S.op('pool' if q % 2 == 0 else 'dve', lambda e, rs=rs, q=q: e.tensor_tensor(
                        Rr[rs][:], bc3(dabf[par][:, c, 4 * q:4 * q + 4], 128),
                        tri.unsqueeze(1).broadcast_to([128, 4, 128]), op=ALU.mult),
                         reads=['dabf%d' % par, 'cst'], writes=['Rr%d' % rs])
                    b = ps_alloc()
                    mm(pv(b), SLm, Rr[rs][:].rearrange("p a l -> p (a l)"), True, True, ['cst', 'Rr%d' % rs], [pn(b)])
                    es_ = nxt('E', 3)
                    S.sz = 512
                    S.op('act', lambda e, b=b, es_=es_: e.activation(out=Eq[es_][:].rearrange("p a l -> p (a l)"), in_=pv(b), func=AF.Exp),
                         reads=[pn(b)], writes=['Eq%d' % es_])
                    ps_rel(b)
                    S.sz = 300
                    S.op('dve', lambda e, es_=es_, q=q: e.tensor_tensor(
                        MT[cp][:, 4 * q:4 * q + 4, :], Eq[es_][:], scm[cp][:, q // 2, :].unsqueeze(1).broadcast_to([128, 4, 128]),
                        op=ALU.mult), reads=['Eq%d' % es_, 'scm%d' % cp], writes=['MT%d_%d' % (cp, q)])
                b = ps_alloc()
                for e_ in range(8):
                    tp(pvb(b)[:, e_ * 128:(e_ + 1) * 128], xbcT[par][:, e_, tk], [XB[e_]], [pn(b)])
                xv = pvb(b).rearrange("p (h q) -> p h q", q=64)
                S.sz = 1024
                S.op('dve', lambda e, xv=xv: e.tensor_tensor(xdt[cp][:].rearrange("p (h q) -> p h q", q=64), xv,
                                                            bc3(dtv[par][:, c, :], 64), op=ALU.mult),
                     reads=[pn(b), 'dtv%d' % par], writes=['xdt%d' % cp])
                ps_rel(b)
                S.sz = 1024
                S.op('dve', lambda e: e.tensor_tensor(xdtd[cp][:].rearrange("p (h q) -> p h q", q=64),
                                                       xdt[cp][:].rearrange("p (h q) -> p h q", q=64),
                                                       bc3(sm[cp][:, 0:16], 64), op=ALU.mult),
                     reads=['xdt%d' % cp, 'sm%d' % cp], writes=['xdtd%d' % cp])
                b = ps_alloc()
                for g in range(2):
                    tp(pvb(b)[:, g * 128:(g + 1) * 128], xbcT[par][:, 8 + g, tk], [XB[8 + g]], [pn(b)])
                S.sz = 256
                S.op('act', lambda e, b=b: e.activation(out=Btok[cp][:], in_=pvb(b)[:, 0:256], func=AF.Copy),
                     reads=[pn(b)], writes=['Btok%d' % cp])
                ps_rel(b)

            def bmain(n, c):
                j, T = divmod(n, NTS)
                par = n % 2
                cp = c
                tk = slice(c * 128, (c + 1) * 128)
                XB = ['xbcT%d_%d' % (par, e_) for e_ in range(12)]
                MTN = ['MT%d_%d' % (cp, q) for q in range(4)]
                if T == 0 and c == 0:
                    S.sz = 1024
                    S.op('pool', lambda e: e.memset(Hs[:], 0.0), writes=['Hs0', 'Hs1'])
                    S.sz = 512
                    S.op('pool', lambda e: e.memset(Hbf[:], 0.0), writes=['Hbf0', 'Hbf1'])
                S.op('pool', lambda e: e.memset(ssy[:, 0:2], 0.0), writes=['ssy'])
                for g in range(2):
                    gs = slice(g * 512, (g + 1) * 512)
                    byd = ps_alloc()
                    for ee in range(4):
                        e_ = 4 * g + ee
                        mm(pv(byd)[:, ee * 128:(ee + 1) * 128], xbcT[par][:, e_, tk], diagD[:, e_, :], ee == 0, True,
                           [XB[e_], 'diagD'], [pn(byd)], **({} if ee == 0 else {'skip_group_check': True}))
                    for hh in range(8):
                        h = 8 * g + hh
                        mm(pv(byd)[:, hh * 64:(hh + 1) * 64], MT[cp][:, h, :], xdt[cp][:, h * 64:(h + 1) * 64],
                           False, hh == 7, [MTN[h // 4], 'xdt%d' % cp], [pn(byd)], skip_group_check=True)
                    bst = ps_alloc()
                    mm(pv(bst), Btok[cp][:, g * 128:(g + 1) * 128], xdtd[cp][:, gs], True, True,
                       ['Btok%d' % cp, 'xdtd%d' % cp], [pn(bst)])
                    byo = ps_alloc()
                    mm(pv(byo), xbcT[par][:, 10 + g, tk], Hbf[:, gs], True, True, [XB[10 + g], 'Hbf%d' % g], [pn(byo)])
                    yg = ybuf[:, gs]
                    ygn = 'ybuf%d' % g
                    S.sz = 512
                    S.op('dve', lambda e, byo=byo, yg=yg, g=g: e.tensor_tensor(
                        yg.rearrange("p (h q) -> p h q", q=64), pv(byo).rearrange("p (h q) -> p h q", q=64),
                        bc3(sm[cp][:, 32 + 8 * g:40 + 8 * g], 64), op=ALU.mult),
                         reads=[pn(byo), 'sm%d' % cp], writes=[ygn])
                    ps_rel(byo)
                    S.sz = 512
                    S.op('dve', lambda e, byd=byd, yg=yg: e.tensor_tensor(yg, pv(byd), yg, op=ALU.add),
                         reads=[pn(byd), ygn], writes=[ygn])
                    ps_rel(byd)
                    S.sz = 512
                    S.op('dve', lambda e, yg=yg, gs=gs: e.tensor_tensor(yg, yg, sz[par][:, c, gs], op=ALU.mult),
                         reads=[ygn, 'sz%d' % c], writes=[ygn])
                    S.sz = 512
                    S.op('act', lambda e, yg=yg, gs=gs, g=g: e.activation(out=yn[:, gs], in_=yg, func=AF.Square,
                                                                          accum_out=ssy[:, g:g + 1]),
                         reads=[ygn, 'ssy'], writes=['yn%d' % g, 'ssy'])
                    Hg = Hs[:, gs]
                    S.sz = 512
                    S.op('pool', lambda e, Hg=Hg, g=g: e.tensor_tensor(
                        Hg.rearrange("p (h q) -> p h q", q=64), Hg.rearrange("p (h q) -> p h q", q=64),
                        bc3(sm[cp][:, 16 + 8 * g:24 + 8 * g], 64), op=ALU.mult),
                         reads=['Hs%d' % g, 'sm%d' % cp], writes=['Hs%d' % g])
                    S.sz = 512
                    S.op('dve', lambda e, Hg=Hg, bst=bst, gs=gs: e.tensor_tensor(Hbf[:, gs], pv(bst), Hg, op=ALU.add),
                         reads=[pn(bst), 'Hs%d' % g], writes=['Hbf%d' % g])
                    S.sz = 512
                    S.op('dve', lambda e, Hg=Hg, bst=bst: e.tensor_tensor(Hg, pv(bst), Hg, op=ALU.add),
                         reads=[pn(bst), 'Hs%d' % g], writes=['Hs%d' % g])
                    ps_rel(bst)
                S.op('pool', lambda e: e.tensor_scalar(ssy[:, 2:4], ssy[:, 0:2], 1.0 / 512.0, EPS, op0=ALU.mult, op1=ALU.add),
                     reads=['ssy'], writes=['ssy'])
                S.op('pool', lambda e: e.tensor_tensor(ssy[:, 2:4], ssy[:, 2:4], mhalf[:, 0:2], op=ALU.pow),
                     reads=['ssy', 'mhalf'], writes=['ssy'])
                for g in range(2):
                    gs = slice(g * 512, (g + 1) * 512)
                    S.sz = 512
                    S.op('act', lambda e, gs=gs, g=g: e.activation(out=yn[:, gs], in_=ybuf[:, gs], func=AF.Identity,
                                                                   scale=ssy[:, 2 + g:3 + g]),
                         reads=['ybuf%d' % g, 'ssy'], writes=['yn%d' % g])
                b = ps_alloc()
                tv = pvb(b).rearrange("p (e t) -> p e t", e=8)
                for e_ in range(8):
                    tp(tv[:, e_, :], yn[:, e_ * 128:(e_ + 1) * 128], ['yn%d' % (e_ // 4)], [pn(b)])
                S.sz = 1024
                S.op('dve', lambda e, tv=tv: e.tensor_tensor(yssdT[:, :, tk], tv, bc3(vecs[:, V_GSSD:V_GSSD + 8], 128), op=ALU.mult),
                     reads=[pn(b), 'vecs'], writes=['yssdT%d' % c])
                ps_rel(b)

            def cstage(n, s):
                j, T = divmod(n, NTS)
                par = n % 2
                tk = slice(s * 128, (s + 1) * 128)
                tok0 = T * TT + s * 128
                sl = nxt('x2', 2)
                xn_ = 'x2_%d' % sl
                S.sz = 524288
                S.dma('sp', x2[sl][:], x_d[j, tok0:tok0 + 128, :], writes=[xn_])
                bo = [ps_alloc(), ps_alloc()]
                for e_ in range(12):
                    if e_ < 4:
                        lt, ln = ypoolT[par][:, e_, tk], 'ypoolT%d' % par
                    else:
                        lt, ln = yssdT[:, e_ - 4, tk], 'yssdT%d' % s
                    for dh in range(2):
                        mm(pv(bo[dh]), lt, w_out[:, e_, dh * 512:(dh + 1) * 512], e_ == 0, e_ == 11,
                           [ln, WOUT[e_ // 4]], [pn(bo[dh])] + (['tick%d' % n] if (e_ == 0 and dh == 0 and s == 0) else []))
                for dh in range(2):
                    ds_ = slice(dh * 512, (dh + 1) * 512)
                    S.sz = 512
                    S.op('dve', lambda e, dh=dh, ds_=ds_: e.tensor_tensor(tbuf[dh][:], pv(bo[dh]), gate_m[:, j, ds_], op=ALU.mult),
                         reads=[pn(bo[dh]), 'gate_m'], writes=['tbuf%d' % dh])
                    ps_rel(bo[dh])
                    S.sz = 512
                    S.op('pool', lambda e, dh=dh, ds_=ds_, sl=sl: e.tensor_tensor(x2[sl][:, ds_], tbuf[dh][:], x2[sl][:, ds_], op=ALU.add),
                         reads=['tbuf%d' % dh, xn_], writes=[xn_])
                r0 = j * SEQ + tok0
                S.sz = 524288
                S.dma('sp', h1s_d[r0:r0 + 128, :], x2[sl][:], reads=[xn_], sem='d_h1st%d' % sl)

            def precast(i):
                S.sz = 3145728
                if i < 8:
                    S.dma('pool', wupbf_d[:, :, i, :], wup_d[i * 128:(i + 1) * 128, :].rearrange("p (b c) -> p b c", b=4),
                          sem='d_precast%d' % i, reads=['tick%d' % i], writes=['wbf%d' % i])
                else:
                    k = i - 8
                    S.dma('pool', wdnbf_d[:, 4 * k:4 * k + 4, :], wdown_d[k * 512:(k + 1) * 512, :].rearrange("(f p) d -> p f d", p=128),
                          sem='d_precast%d' % i, reads=['tick%d' % i], writes=['wbf%d' % i])
            prep_a(0)
            prep_b(0)
            afm(0)
            atm(0)
            prep_a(1)
            prep_b(1)
            for n in range(NT):
                bprep(n, 0)
                bprep(n, 1)
                if n + 1 < NT:
                    afm(n + 1)
                bmain(n, 0)
                bmain(n, 1)
                if n + 1 < NT:
                    atm(n + 1)
                if n + 2 < NT:
                    prep_a(n + 2)
                    prep_b(n + 2)
                cstage(n, 0)
                cstage(n, 1)
                precast(n)
            ps_ring[0] = None
            S.barrier()

        with ExitStack() as esF:
            w_up = sbt(esF, "w_up", [128, 4, 8, D], BF16)
            w_down = sbt(esF, "w_down", [128, 32, D], BF16)
            gate_f = sbt(esF, "gate_f", [128, 2, D])
            gfin = sbt(esF, "gfin", [128, D])
            fT = sbt(esF, "fT", [128, 32, TT], BF16)
            crepF = sbt(esF, "crepF", [128, 16, 128], BF16)
            u2T = [sbt(esF, "u2T%d" % i, [128, 8, TT], BF16) for i in range(2)]
            hs = [sbt(esF, "hs%d" % i, [128, D]) for i in range(4)]
            hn = [sbt(esF, "hn%d" % i, [128, D], BF16) for i in range(2)]
            rt = [sbt(esF, "rt%d" % i, [128, 2, TT]) for i in range(2)]
            tbf = [sbt(esF, "tbf%d" % i, [128, 512]) for i in range(2)]
            ssf = sbt(esF, "ssf", [128, 32])

            WUP = ['w_up%d' % k for k in range(4)]
            fcnt = {'hs': 0, 'rt': 0, 'hn': 0}
            fprepT = {}
            hslot = {}

            def fnxt(key, n_):
                v = fcnt[key] % n_
                fcnt[key] += 1
                return v

            def fprep(m):
                j, T = divmod(m, NTS)
                banks = [ps_alloc(), ps_alloc()]
                hslot[m] = []
                for s in range(2):
                    sl = fnxt('hs', 4)
                    hl = fnxt('hn', 2)
                    hslot[m].append(sl)
                    r0 = j * SEQ + T * TT + s * 128
                    c0 = 4 * sl
                    S.sz = 524288
                    S.dma('sp', hs[sl][:], h1s_d[r0:r0 + 128, :], writes=['hs%d' % sl])
                    S.op('pool', lambda e, c0=c0: e.memset(ssf[:, c0:c0 + 1], 0.0), writes=['ssf%d' % sl])
                    S.sz = 1024
                    S.op('act', lambda e, sl=sl, hl=hl, c0=c0: e.activation(out=hn[hl][:], in_=hs[sl][:], func=AF.Square,
                                                                            accum_out=ssf[:, c0:c0 + 1]),
                         reads=['hs%d' % sl, 'ssf%d' % sl], writes=['hn%d' % hl, 'ssf%d' % sl])
                    S.op('pool', lambda e, c0=c0: e.tensor_scalar(ssf[:, c0 + 1:c0 + 2], ssf[:, c0:c0 + 1], 1024.0 * EPS, None, op0=ALU.add),
                         reads=['ssf%d' % sl], writes=['ssf%d' % sl])
                    S.op('pool', lambda e, c0=c0: e.tensor_tensor(ssf[:, c0 + 2:c0 + 3], ssf[:, c0 + 1:c0 + 2], mhalf[:, 0:1], op=ALU.pow),
                         reads=['ssf%d' % sl, 'mhalf'], writes=['ssf%d' % sl])
                    S.sz = 1024
                    S.op('act', lambda e, sl=sl, hl=hl, c0=c0: e.activation(out=hn[hl][:], in_=hs[sl][:], func=AF.Copy, scale=ssf[:, c0 + 2:c0 + 3]),
                         reads=['hs%d' % sl, 'ssf%d' % sl], writes=['hn%d' % hl])
                    for k in range(8):
                        b = banks[k // 4]
                        dst = pvb(b).rearrange("p (k t) -> p k t", k=4)[:, k % 4, s * 128:(s + 1) * 128]
                        tp(dst, hn[hl][:, k * 128:(k + 1) * 128], ['hn%d' % hl], [pn(b)])
                up_ = m % 2
                for k in range(8):
                    b = banks[k // 4]
                    src = pvb(b).rearrange("p (k t) -> p k t", k=4)[:, k % 4, :]
                    S.sz = 256
                    S.op('act', lambda e, k=k, src=src: e.activation(out=u2T[up_][:, k, :], in_=src, func=AF.Identity,
                                                                     scale=G2[:, k, j:j + 1], bias=S2[:, k, j:j + 1]),
                         reads=[pn(b), 'G2', 'S2'], writes=['u2T%d' % up_])
                ps_rel(banks[0])
                ps_rel(banks[1])

            def fup(m):
                up_ = m % 2
                for fp in range(16):
                    b = ps_alloc()
                    bv = pv(b).rearrange("p (a t) -> p a t", a=2)
                    for half in range(2):
                        f = 2 * fp + half
                        for k in range(8):
                            mm(bv[:, half, :], w_up[:, f // 8, k, (f % 8) * 128:(f % 8 + 1) * 128], u2T[up_][:, k, :], k == 0, k == 7,
                               [WUP[f // 8], 'u2T%d' % up_], [pn(b)])
                    rl = fnxt('rt', 2)
                    S.sz = 512
                    S.op('act', lambda e, rl=rl, bv=bv: e.activation(out=rt[rl][:], in_=bv, func=AF.Relu),
                         reads=[pn(b)], writes=['rt%d' % rl])
                    ps_rel(b)
                    eng = 'pool' if fp % 3 == 2 else 'dve'
                    S.sz = 512
                    S.op(eng, lambda e, rl=rl, fp=fp: e.tensor_tensor(fT[:, 2 * fp:2 * fp + 2, :], rt[rl][:], rt[rl][:], op=ALU.mult),
                         reads=['rt%d' % rl], writes=['fT'])

            def fdown(m, s):
                j, T = divmod(m, NTS)
                tk = slice(s * 128, (s + 1) * 128)
                sl = hslot[m][s]
                hn_ = 'hs%d' % sl
                c0 = 4 * sl
                bd = [ps_alloc(), ps_alloc()]
                for f in range(32):
                    for dh in range(2):
                        mm(pv(bd[dh]), fT[:, f, tk], w_down[:, f, dh * 512:(dh + 1) * 512], f == 0, f == 31,
                           ['fT', 'w_down%d' % (f // 8)], [pn(bd[dh])])
                for dh in range(2):
                    ds_ = slice(dh * 512, (dh + 1) * 512)
                    S.sz = 512
                    S.op('dve', lambda e, dh=dh, ds_=ds_: e.tensor_tensor(tbf[dh][:], pv(bd[dh]), gate_f[:, j, ds_], op=ALU.mult),
                         reads=[pn(bd[dh]), 'gate_f'], writes=['tbf%d' % dh])
                    ps_rel(bd[dh])
                    S.sz = 512
                    S.op('pool', lambda e, dh=dh, ds_=ds_: e.tensor_tensor(hs[sl][:, ds_], tbf[dh][:], hs[sl][:, ds_], op=ALU.add),
                         reads=['tbf%d' % dh, hn_], writes=[hn_])
                hl = fnxt('hn', 2)
                S.op('pool', lambda e: e.memset(ssf[:, c0:c0 + 1], 0.0), writes=['ssf%d' % sl])
                S.sz = 1024
                S.op('act', lambda e: e.activation(out=hn[hl][:], in_=hs[sl][:], func=AF.Square, accum_out=ssf[:, c0:c0 + 1]),
                     reads=[hn_, 'ssf%d' % sl], writes=['hn%d' % hl, 'ssf%d' % sl])
                S.op('pool', lambda e: e.tensor_scalar(ssf[:, c0 + 1:c0 + 2], ssf[:, c0:c0 + 1], 1.0 / 1024.0, EPS, op0=ALU.mult, op1=ALU.add),
                     reads=['ssf%d' % sl], writes=['ssf%d' % sl])
                S.op('pool', lambda e: e.tensor_tensor(ssf[:, c0 + 2:c0 + 3], ssf[:, c0 + 1:c0 + 2], mhalf[:, 0:1], op=ALU.pow),
                     reads=['ssf%d' % sl, 'mhalf'], writes=['ssf%d' % sl])
                S.sz = 1024
                S.op('dve', lambda e: e.scalar_tensor_tensor(out=hs[sl][:], in0=hs[sl][:], scalar=ssf[:, c0 + 2:c0 + 3], in1=gfin[:],
                                                             op0=ALU.mult, op1=ALU.mult),
                     reads=[hn_, 'ssf%d' % sl, 'gfin'], writes=[hn_])
                tok0 = T * TT + s * 128
                S.sz = 524288
                S.dma('sp', out_d[j, tok0:tok0 + 128, :], hs[sl][:], reads=[hn_], sem='d_out%d' % sl)

            fprep(0)
            for fb in range(4):
                S.sz = 2097152
                S.dma('sp', w_up[:, fb, :, :], wupbf_d[:, fb, :, :], writes=['w_up%d' % fb])
            make_crep(crepF)
            for j in range(2):
                S.sz = 524288
                S.dma('sp', gate_f[:, j, :], gbias_d[:, 1024:2048], writes=['gate_f'], sem='d_gf%d' % j)
            S.sz = 524288
            S.dma('sp', gfin[:], gfin_d, writes=['gfin'])
            wadaF = fT[:].rearrange("p f t -> p (f t)").rearrange("p (k c) -> p k c", k=8)
            ada_block(5, wadaF, 'fT', crepF, gate=(gate_f, 'gate_f'))
            for i in range(4):
                S.sz = 2097152
                S.dma('sp', w_down[:, 8 * i:8 * i + 8, :], wdnbf_d[:, 8 * i:8 * i + 8, :], writes=['w_down%d' % i])
            for m in range(NT):
                fup(m)
                if m + 1 < NT:
                    fprep(m + 1)
                fdown(m, 0)
                fdown(m, 1)
            S.barrier()
    return nc


_NC_CACHE = {}


def _consts():
    i = np.arange(128)
    ident = np.eye(128, dtype=np.float32)
    tri = (i[:, None] <= i[None, :]).astype(np.float32)
    sl = (i[:, None] > i[None, :]).astype(np.float32)
    ones = np.ones((128, 128), np.float32)
    return np.ascontiguousarray(np.concatenate([ident, tri, sl, ones], axis=1))


def _fm(v, nchunk):
    return np.ascontiguousarray(np.asarray(v, np.float32).reshape(nchunk, 128).T)


def make_in_maps(x, c, w_ada, b_ada, g_mix, w_in, conv_w, conv_b, dt_bias, a_log, d_skip, g_ssd,
                 w_pool, pool_scale, w_out, g_mlp, w_up, w_down, g_final):
    f = lambda a: np.ascontiguousarray(np.asarray(a, dtype=np.float32))
    x = f(x); c = f(c)
    w_ada0 = f(w_ada[0]); b0 = f(b_ada[0])
    consts = _consts()
    invc = np.zeros((128, 4, 16), np.float32)
    for g, w in enumerate(POOL_W):
        invc[:, g, :] = 1.0 / np.minimum(np.arange(1, 17), w)
    cw = f(conv_w[0])
    cw_fm = np.ascontiguousarray(cw.T.reshape(12, 128, 4).transpose(1, 0, 2)).reshape(128, 48)
    gbias = np.ascontiguousarray(np.broadcast_to(np.concatenate([b0[2048:3072], b0[5120:6144]])[None, :], (128, 2048)))
    gfin = np.ascontiguousarray(np.broadcast_to(f(g_final)[None, :], (128, D)))
    rowv = np.ascontiguousarray(np.broadcast_to(
        np.concatenate([f(dt_bias[0]), f(a_log[0]), f(d_skip[0])])[None, :], (128, 48)))
    shared = dict(consts=consts, gbias=gbias, gfin=gfin, rowv=rowv, w_ada=w_ada0, w_in=f(w_in[0]),
                  w_out=f(w_out[0]), w_up=f(w_up[0]), w_down=f(w_down[0]),
                  w_pool=f(w_pool[0]).reshape(512, 128))
    in_maps = []
    for core in range(NCORES):
        vecs = np.zeros((128, NV), np.float32)
        vecs[:, V_GMIX:V_GMIX + 8] = _fm(g_mix[0], 8)
        vecs[:, V_GMLP:V_GMLP + 8] = _fm(g_mlp[0], 8)
        vecs[:, V_CW:V_CW + 48] = cw_fm
        vecs[:, V_CB:V_CB + 12] = _fm(conv_b[0], 12)
        vecs[:, V_PSC:V_PSC + 4] = _fm(pool_scale[0], 4)
        vecs[:, V_GSSD:V_GSSD + 8] = _fm(g_ssd[0], 8)
        vecs[:, V_BT:V_BT + 48] = _fm(b0, 48)
        vecs[:, V_INVC:V_INVC + 64] = invc.reshape(128, 64)
        vecs[:, V_DSK:V_DSK + 8] = _fm(np.repeat(f(d_skip[0]), 64), 8)
        cc = c[NB * core:NB * core + NB]
        vecs[:, V_CT:V_CT + 16] = np.ascontiguousarray(cc.reshape(NB, 8, 128).transpose(2, 1, 0)).reshape(128, 16)
        m = dict(shared)
        m["x"] = np.ascontiguousarray(x[NB * core:NB * core + NB])
        m["vecs"] = vecs
        in_maps.append(m)
    return in_maps


def kernel(**inputs):
    if "nc" not in _NC_CACHE:
        _NC_CACHE["nc"] = build_nc()
    nc = _NC_CACHE["nc"]
    in_maps = make_in_maps(**inputs)
    res = run_bass_kernel_spmd(nc, in_maps, core_ids=list(range(NCORES)))
    out = np.concatenate([np.asarray(r["out"]) for r in res.results], axis=0)
    return out.astype(np.float32)
```

```python
import types
import numpy as np
from contextlib import ExitStack
import concourse.bass as bass
import concourse.mybir as mybir
from concourse.bass_utils import run_bass_kernel_spmd

F32 = mybir.dt.float32
BF16 = mybir.dt.bfloat16
AF = mybir.ActivationFunctionType
ALU = mybir.AluOpType

NCORES = 8
D = 1024
SEQ = 2048
NB = 2
TT = 256
NTS = SEQ // TT
NT = NB * NTS
EPS = 1e-5
IN_W = 3088
OFF_Z = 512
OFF_XBC = 1536
OFF_DT = 3072
POOL_W = (2, 4, 8, 16)

V_GMIX, V_GMLP, V_CW, V_CB, V_PSC, V_GSSD, V_BT, V_DTB, V_ALOG, V_DSK, V_INVC, V_CT = (
    0, 8, 16, 64, 76, 80, 88, 136, 152, 168, 184, 248)
NV = 264


def _freeze(fn):
    if fn.__closure__ is None:
        return fn
    cells = tuple(types.CellType(c.cell_contents) for c in fn.__closure__)
    return types.FunctionType(fn.__code__, fn.__globals__, fn.__name__, fn.__defaults__, cells)


def _cost(e, n):
    if e == 'pe':
        return 6.0 + 0.45 * n + (0.07 * (n - 256) if n > 256 else 0.0)
    if e == 'act':
        return 40.0 + (250.0 + n) / 1.5
    if e == 'dve':
        return (180.0 + n) / 0.96
    if e == 'pool':
        return 200.0 + 1.8 * n
    return 60.0


class Sched:
    ENG = ('pe', 'act', 'dve', 'pool', 'sp')
    DMA_BW = 280.0
    DMA_LAT = 2000.0
    XLAT = 600.0
    SLAT = 120.0
    XLAT_SEG = {1: 500.0}

    def __init__(self, nc, es):
        self.nc = nc
        self.es = es
        self.eng = {'pe': nc.tensor, 'act': nc.scalar, 'dve': nc.vector,
                    'pool': nc.gpsimd, 'sp': nc.sync}
        self.sem = {e: es.enter_context(nc.semaphore("c_" + e)) for e in self.ENG}
        self.cnt = {e: 0 for e in self.ENG}
        self.seen = {e: {} for e in self.ENG}
        self.dma_sems = {}
        self.res = {}
        self.ops = []
        self.sim_total = 0.0
        self.sz = None

    def _r(self, name):
        st = self.res.get(name)
        if st is None:
            st = {'w': None, 'r': []}
            self.res[name] = st
        return st

    def _record(self, rec, reads, writes):
        ops = self.ops
        e = rec['e']
        is_dma = rec['dma']
        deps = {}

        def add(d, raw):
            o = ops[d]
            need = is_dma or o['dma'] or o['e'] != e or e != 'pe'
            deps[d] = deps.get(d, False) or need

        for r in reads:
            w = self._r(r)['w']
            if w is not None:
                add(w, True)
        for wn in writes:
            st = self._r(wn)
            if st['w'] is not None:
                add(st['w'], False)
            for d in st['r']:
                add(d, False)
        rec['deps'] = deps
        rec['id'] = len(ops)
        ops.append(rec)
        for r in reads:
            self._r(r)['r'].append(rec['id'])
        for wn in writes:
            st = self._r(wn)
            st['w'] = rec['id']
            st['r'] = []

    def op(self, e, fn, reads=(), writes=(), n=None):
        if n is None:
            n = self.sz if self.sz is not None else 64
        self.sz = None
        self._record({'e': e, 'dma': False, 'fn': _freeze(fn), 'cost': _cost(e, n)}, reads, writes)

    def dma(self, q, out, in_, reads=(), writes=(), sem=None, nbytes=None):
        if nbytes is None:
            nbytes = self.sz if self.sz is not None else 65536
        self.sz = None
        name = sem or ('d_' + (writes[0] if writes else reads[0]))
        if name not in self.dma_sems:
            self.dma_sems[name] = [self.es.enter_context(self.nc.semaphore(name)), 0]
        self._record({'e': q, 'dma': True, 'out': out, 'in_': in_, 'sem': name, 'cost': 80.0,
                      'nbytes': nbytes}, reads, writes)

    INORDER_SEGS = (0,)
    seg_idx = 0

    def _schedule(self):
        ops = self.ops
        nops = len(ops)
        seg = self.seg_idx
        self.seg_idx += 1
        xlat = self.XLAT_SEG.get(seg, self.XLAT)
        if seg in self.INORDER_SEGS:
            order = {e: [] for e in self.ENG}
            for o in ops:
                order[o['e']].append(o['id'])
            return order
        succ = [[] for _ in range(nops)]
        nun = [0] * nops
        for o in ops:
            nun[o['id']] = len(o['deps'])
            for d in o['deps']:
                succ[d].append(o['id'])
        ready = {e: [] for e in self.ENG}
        rtime = [0.0] * nops
        fin = [0.0] * nops
        for o in ops:
            if nun[o['id']] == 0:
                ready[o['e']].append(o['id'])
        efree = {e: 0.0 for e in self.ENG}
        dma_free = 0.0
        order = {e: [] for e in self.ENG}
        done = 0
        while done < nops:
            best = None
            for e in self.ENG:
                rl = ready[e]
                if not rl:
                    continue
                tmin = min(rtime[i] for i in rl)
                t = max(efree[e], tmin)
                if best is None or t < best[0]:
                    best = (t, e)
            t, e = best
            rl = ready[e]
            cand = min(i for i in rl if rtime[i] <= t)
            rl.remove(cand)
            o = ops[cand]
            start = t
            efree[e] = start + o['cost']
            if o['dma']:
                st = max(efree[e], dma_free)
                dma_free = st + o['nbytes'] / self.DMA_BW
                fin[cand] = dma_free + self.DMA_LAT
            else:
                fin[cand] = efree[e]
            order[e].append(cand)
            done += 1
            for s_ in succ[cand]:
                nun[s_] -= 1
                lat = fin[cand] + (xlat if (ops[s_]['e'] != e or o['dma']) else self.SLAT)
                if lat > rtime[s_]:
                    rtime[s_] = lat
                if nun[s_] == 0:
                    ready[ops[s_]['e']].append(s_)
        self.sim_total += max(fin) if nops else 0.0
        return order

    def flush(self):
        ops = self.ops
        if not ops:
            return
        order = self._schedule()
        pos = [0] * len(ops)
        for e in self.ENG:
            for p_, i in enumerate(order[e]):
                pos[i] = p_
        waited = [False] * len(ops)
        sel = [None] * len(ops)
        for o in ops:
            best = {}
            lst = []
            for d, w in o['deps'].items():
                if not w:
                    continue
                od = ops[d]
                if od['dma']:
                    lst.append(d)
                else:
                    k = od['e']
                    if k not in best or pos[best[k]] < pos[d]:
                        best[k] = d
            lst.extend(best.values())
            sel[o['id']] = lst
            for d in lst:
                waited[d] = True
        for e in self.ENG:
            nd = [i for i in order[e] if not ops[i]['dma']]
            if nd:
                waited[nd[-1]] = True
        tok = [None] * len(ops)
        for e in self.ENG:
            c = self.cnt[e]
            dc = {}
            for i in order[e]:
                o = ops[i]
                if o['dma']:
                    rec = self.dma_sems[o['sem']]
                    v = dc.get(o['sem'], rec[1]) + 16
                    dc[o['sem']] = v
                    tok[i] = (o['sem'], v)
                elif waited[i]:
                    c += 1
                    tok[i] = (e, c)
        for e in self.ENG:
            eng = self.eng[e]
            seen = self.seen[e]
            for i in order[e]:
                o = ops[i]
                need = {}
                for d in sel[i]:
                    k, v = tok[d]
                    if need.get(k, 0) < v:
                        need[k] = v
                for k, v in need.items():
                    if seen.get(k, 0) < v:
                        eng.wait_ge(self.sem[k] if k in self.sem else self.dma_sems[k][0], v)
                        seen[k] = v
                if o['dma']:
                    ins = eng.dma_start(out=o['out'], in_=o['in_'])
                    rec = self.dma_sems[o['sem']]
                    rec[1] += 16
                    assert rec[1] == tok[i][1]
                    ins.then_inc(rec[0], 16)
                else:
                    ins = o['fn'](eng)
                    if waited[i]:
                        self.cnt[e] += 1
                        assert self.cnt[e] == tok[i][1]
                        ins.then_inc(self.sem[e], 1)
        self.ops = []
        self.res = {}

    def wait_all(self, e):
        for k in self.ENG:
            if k != e and self.cnt[k] > self.seen[e].get(k, 0):
                self.eng[e].wait_ge(self.sem[k], self.cnt[k])
                self.seen[e][k] = self.cnt[k]
        for k, (s, v) in self.dma_sems.items():
            if v > self.seen[e].get(k, 0):
                self.eng[e].wait_ge(s, v)
                self.seen[e][k] = v

    def barrier(self):
        self.flush()
        for e in self.ENG:
            self.wait_all(e)


def build_nc(debug=False):
    nc = bass.Bass("TRN2", target_bir_lowering=False)

    def din(name, shape):
        return nc.dram_tensor(name, shape, F32, kind="ExternalInput").ap()

    x_d = din("x", [NB, SEQ, D])
    vecs_d = din("vecs", [128, NV])
    consts_d = din("consts", [128, 512])
    gbias_d = din("gbias", [128, 2048])
    gfin_d = din("gfin", [128, D])
    rowv_d = din("rowv", [128, 48])
    wada_d = din("w_ada", [D, 6 * D])
    win_d = din("w_in", [D, IN_W])
    wout_d = din("w_out", [1536, D])
    wup_d = din("w_up", [D, 4 * D])
    wdown_d = din("w_down", [4 * D, D])
    wpool_d = din("w_pool", [512, 128])
    out_d = nc.dram_tensor("out", [NB, SEQ, D], F32, kind="ExternalOutput").ap()
    wupbf_d = nc.dram_tensor("wup_bf", [128, 4, 8, D], BF16, kind="Internal").ap()
    wdnbf_d = nc.dram_tensor("wdn_bf", [128, 32, D], BF16, kind="Internal").ap()
    h1s_d = nc.dram_tensor("h1s", [NB * SEQ, D], F32,
                           kind="ExternalOutput" if debug else "Internal").ap()

    with ExitStack() as es0:
        S = Sched(nc, es0)

        def sbt(es, name, shape, dt=F32):
            return es.enter_context(nc.sbuf_tensor("s_" + name, shape, dt))

        psb = [es0.enter_context(nc.psum_tensor("psb%d" % i, [128, 512], F32)) for i in range(8)]
        ps_free = [True] * 8
        ps_next = [0]

        ps_ring = [None]
        ps_nx = {}

        def ps_alloc(ring=None):
            lo, hi = ring or ps_ring[0] or (0, 8)
            n_ = hi - lo
            st = ps_nx.get((lo, hi), 0)
            for t in range(n_):
                i = lo + (st + t) % n_
                if ps_free[i]:
                    ps_free[i] = False
                    ps_nx[(lo, hi)] = (i - lo + 1) % n_
                    return i
            raise RuntimeError("no free PSUM bank")

        def ps_rel(i):
            ps_free[i] = True

        def pv(i):
            return psb[i][:]

        def pvb(i):
            return psb[i][:].bitcast(BF16)

        def pn(i):
            return "ps%d" % i

        vecs = sbt(es0, "vecs", [128, NV])
        rowv = sbt(es0, "rowv", [128, 48])
        cst = sbt(es0, "cst", [128, 512], BF16)
        ident = cst[:, 0:128]
        tri = cst[:, 128:256]
        SLm = cst[:, 256:384]
        ones = cst[:, 384:512]
        cact = sbt(es0, "cact", [128, 16], BF16)
        G1 = sbt(es0, "G1", [128, 8, 2])
        S1 = sbt(es0, "S1", [128, 8, 2])
        G2 = sbt(es0, "G2", [128, 8, 2])
        S2 = sbt(es0, "S2", [128, 8, 2])
        gm32 = sbt(es0, "gm32", [128, 16])
        a_bc = sbt(es0, "a_bc", [128, 16])
        mhalf = sbt(es0, "mhalf", [128, 4])
        smallt = sbt(es0, "smallt", [128, 16])

        S.dma('sp', vecs[:], vecs_d, writes=['vecs'])
        S.dma('sp', rowv[:], rowv_d, writes=['rowv'])
        S.sz = 262144
        S.dma('pool', cst[:], consts_d, writes=['cst'])
        S.op('pool', lambda e: e.memset(mhalf[:], -0.5), writes=['mhalf'])

        def mm(out, lhsT, rhs, start, stop, R, W, **kw):
            S.sz = int(np.prod(rhs.shape[1:]))
            S.op('pe', lambda e: e.matmul(out, lhsT, rhs, start=start, stop=stop, **kw),
                 reads=R, writes=W)

        def tp(out, in_, R, W):
            S.sz = 128
            S.op('pe', lambda e: e.transpose(out, in_, ident), reads=R + ['cst'], writes=W)

        S.op('act', lambda e: e.activation(out=cact[:], in_=vecs[:, V_CT:V_CT + 16], func=AF.Silu),
             reads=['vecs'], writes=['cact'])
        S.op('act', lambda e: e.activation(out=a_bc[:], in_=rowv[:, 16:32], func=AF.Exp),
             reads=['rowv'], writes=['a_bc'])
        S.op('dve', lambda e: e.tensor_scalar(a_bc[:], a_bc[:], -1.0, None, op0=ALU.mult),
             reads=['a_bc'], writes=['a_bc'])
        S.op('dve', lambda e: e.tensor_scalar(gm32[:], vecs[:, V_GMIX:V_GMIX + 16], 32.0, None, op0=ALU.mult),
             reads=['vecs'], writes=['gm32'])

        def bc3(ap2, n):
            return ap2.unsqueeze(2).broadcast_to([ap2.shape[0], ap2.shape[1], n])

        def ada_block(blk, wbuf, wname, crep, gate=None):
            S.sz = 4194304
            S.dma('pool', wbuf[:], wada_d[:, blk * 1024:(blk + 1) * 1024].rearrange("(k p) c -> p k c", p=128),
                  writes=[wname])
            if gate is None:
                b = ps_alloc()
                pm = pv(b)[:, 0:16].rearrange("p (e j) -> p e j", j=2)
                for e_ in range(8):
                    for k in range(8):
                        mm(pm[:, e_, :], wbuf[:, k, e_ * 128:(e_ + 1) * 128], cact[:, 2 * k:2 * k + 2],
                           k == 0, k == 7, [wname, 'cact'], [pn(b)])
                bt = bc3(vecs[:, V_BT + blk * 8:V_BT + blk * 8 + 8], 2)
                dst, dn = {0: (S1, 'S1'), 1: (G1, 'G1'), 3: (S2, 'S2'), 4: (G2, 'G2')}[blk]
                S.sz = 16
                S.op('dve', lambda e: e.tensor_tensor(dst[:], pm, bt, op=ALU.add),
                     reads=[pn(b), 'vecs'], writes=[dn])
                ps_rel(b)
                if blk in (1, 4):
                    gsel = gm32[:, 0:8] if blk == 1 else gm32[:, 8:16]
                    S.sz = 16
                    S.op('dve', lambda e: e.scalar_tensor_tensor(out=dst[:], in0=dst[:], scalar=1.0,
                                                                 in1=bc3(gsel, 2), op0=ALU.add, op1=ALU.mult),
                         reads=[dn, 'gm32'], writes=[dn])
            else:
                gbuf, gname = gate
                for j in range(2):
                    for dh in range(2):
                        b = ps_alloc()
                        for k in range(8):
                            mm(pv(b), crep[:, 2 * k + j, :], wbuf[:, k, dh * 512:(dh + 1) * 512],
                               k == 0, k == 7, [wname, 'crep'], [pn(b)])
                        gs = gbuf[:, j, dh * 512:(dh + 1) * 512]
                        S.sz = 512
                        S.op('dve', lambda e, gs=gs, b=b: e.tensor_tensor(gs, pv(b), gs, op=ALU.add),
                             reads=[pn(b), gname], writes=[gname])
                        ps_rel(b)

        def make_crep(crep):
            S.sz = 2048
            S.op('dve', lambda e: e.tensor_copy(crep[:], bc3(cact[:], 128)), reads=['cact'], writes=['crep'])

        with ExitStack() as esM:
            w_in = sbt(esM, "w_in", [128, 8, IN_W], BF16)
            w_out = sbt(esM, "w_out", [128, 12, D], BF16)
            w_pool = sbt(esM, "w_pool", [128, 4, 128], BF16)
            gate_m = sbt(esM, "gate_m", [128, 2, D])
            diagC = sbt(esM, "diagC", [128, 48, 128], BF16)
            diagD = sbt(esM, "diagD", [128, 8, 128], BF16)
            S.sz = 6144
            S.op('dve', lambda e: e.tensor_tensor(diagC[:], ident.unsqueeze(1).broadcast_to([128, 48, 128]),
                                                  bc3(vecs[:, V_CW:V_CW + 48], 128), op=ALU.mult),
                 reads=['cst', 'vecs'], writes=['diagC'])
            S.sz = 1024
            S.op('dve', lambda e: e.tensor_tensor(diagD[:], ident.unsqueeze(1).broadcast_to([128, 8, 128]),
                                                  bc3(vecs[:, V_DSK:V_DSK + 8], 128), op=ALU.mult),
                 reads=['cst', 'vecs'], writes=['diagD'])

            with ExitStack() as esP:
                wada = [sbt(esP, "wada%d" % i, [128, 8, 1024], BF16) for i in range(2)]
                crep = sbt(esP, "crep", [128, 16, 128], BF16)
                make_crep(crep)
                for j in range(2):
                    S.sz = 524288
                    S.dma('sp', gate_m[:, j, :], gbias_d[:, 0:1024], writes=['gate_m'], sem='d_gm%d' % j)
                ada_block(0, wada[0], 'wada0', crep)
                ada_block(1, wada[1], 'wada1', crep)
                for k in range(8):
                    S.sz = 1581056
                    S.dma('pool', w_in[:, k, :], win_d[k * 128:(k + 1) * 128, :], writes=['w_in%d' % k])
                S.sz = 262144
                S.dma('pool', w_pool[:], wpool_d.rearrange("(g p) d -> p g d", p=128), writes=['w_pool'])
                ada_block(2, wada[0], 'wada0', crep, gate=(gate_m, 'gate_m'))
                for i in range(3):
                    S.sz = 2097152
                    S.dma('pool', w_out[:, 4 * i:4 * i + 4, :],
                          wout_d[i * 512:(i + 1) * 512, :].rearrange("(e p) d -> p e d", p=128),
                          writes=['w_out%d' % i])
                ada_block(3, wada[1], 'wada1', crep)
                ada_block(4, wada[0], 'wada0', crep)
                S.barrier()
            WIN = ['w_in%d' % k for k in range(8)]
            WOUT = ['w_out%d' % i for i in range(3)]

            xt = [sbt(esM, "xt%d" % i, [128, D]) for i in range(2)]
            xn = [sbt(esM, "xn%d" % i, [128, D], BF16) for i in range(2)]
            ssn = sbt(esM, "ssn", [128, 8])
            uTb = [sbt(esM, "uT%d" % i, [128, 8, TT], BF16) for i in range(2)]
            up = [sbt(esM, "up%d" % i, [128, 2, 16 + TT]) for i in range(2)]
            ta = [sbt(esM, "ta%d" % i, [128, 16 + TT]) for i in range(2)]
            tb = [sbt(esM, "tb%d" % i, [128, 16 + TT]) for i in range(2)]
            t16 = sbt(esM, "t16", [128, 4, 16])
            pbf = sbt(esM, "pbf", [128, 4, TT], BF16)
            pool_halo = sbt(esM, "pool_halo", [128, 4, 16])
            pre = [sbt(esM, "pre%d" % i, [128, 2, 4 + TT], BF16) for i in range(3)]
            conv_halo = sbt(esM, "conv_halo", [128, 12, 3], BF16)
            xbcT = [sbt(esM, "xbcT%d" % i, [128, 12, TT], BF16) for i in range(2)]
            sz = [sbt(esM, "sz%d" % i, [128, 2, D], BF16) for i in range(1)] * 2
            dtb = sbt(esM, "dtb", [128, 2, 16])
            dtv = [sbt(esM, "dtv%d" % i, [128, 2, 16]) for i in range(2)]
            dabf = [sbt(esM, "dabf%d" % i, [128, 2, 16], BF16) for i in range(2)]
            Rr = [sbt(esM, "Rr%d" % i, [128, 4, 128], BF16) for i in range(4)]
            Eq = [sbt(esM, "Eq%d" % i, [128, 4, 128], BF16) for i in range(3)]
            MT = [sbt(esM, "MT%d" % i, [128, 16, 128], BF16) for i in range(2)]
            scm = [sbt(esM, "scm%d" % i, [128, 2, 128], BF16) for i in range(2)]
            sm = [sbt(esM, "sm%d" % i, [128, 48]) for i in range(2)]
            xdt = [sbt(esM, "xdt%d" % i, [128, D], BF16) for i in range(2)]
            xdtd = [sbt(esM, "xdtd%d" % i, [128, D], BF16) for i in range(2)]
            Btok = [sbt(esM, "Btok%d" % i, [128, 256], BF16) for i in range(2)]
            Hs = sbt(esM, "Hs", [128, D])
            Hbf = sbt(esM, "Hbf", [128, D], BF16)
            ybuf = sbt(esM, "ybuf", [128, D])
            yn = sbt(esM, "yn", [128, D], BF16)
            ssy = sbt(esM, "ssy", [128, 4])
            ypoolT = [sbt(esM, "ypoolT%d" % i, [128, 4, TT], BF16) for i in range(2)]
            yssdT = sbt(esM, "yssdT", [128, 8, TT], BF16)
            tbuf = [sbt(esM, "tbuf%d" % i, [128, 512]) for i in range(2)]
            x2 = [sbt(esM, "x2_%d" % i, [128, D]) for i in range(2)]
            cnt = {'xt': 0, 'pre': 0, 'acc': 0, 'up': 0, 'R': 0, 'E': 0, 'x2': 0}

            def nxt(key, n):
                v = cnt[key] % n
                cnt[key] += 1
                return v

            prepT = {}

            def prep_a(n):
                j, T = divmod(n, NTS)
                banks = [ps_alloc(), ps_alloc()]
                prepT[n] = banks
                for s in range(2):
                    sl = nxt('xt', 2)
                    tok0 = T * TT + s * 128
                    S.sz = 524288
                    S.dma('sp', xt[sl][:], x_d[j, tok0:tok0 + 128, :], writes=['xt%d' % sl])
                    S.op('pool', lambda e, sl=sl: e.memset(ssn[:, 4 * sl:4 * sl + 1], 0.0), writes=['ssn%d' % sl])
                    S.sz = 1024
                    S.op('act', lambda e, sl=sl: e.activation(out=xn[sl][:], in_=xt[sl][:], func=AF.Square,
                                                              accum_out=ssn[:, 4 * sl:4 * sl + 1]),
                         reads=['xt%d' % sl, 'ssn%d' % sl], writes=['xn%d' % sl, 'ssn%d' % sl])
                    S.op('pool', lambda e, sl=sl: e.tensor_scalar(ssn[:, 4 * sl + 1:4 * sl + 2], ssn[:, 4 * sl:4 * sl + 1],
                                                                  1024.0 * EPS, None, op0=ALU.add),
                         reads=['ssn%d' % sl], writes=['ssn%d' % sl])
                    S.op('pool', lambda e, sl=sl: e.tensor_tensor(ssn[:, 4 * sl + 2:4 * sl + 3], ssn[:, 4 * sl + 1:4 * sl + 2],
                                                                  mhalf[:, 0:1], op=ALU.pow),
                         reads=['ssn%d' % sl, 'mhalf'], writes=['ssn%d' % sl])
                    S.sz = 1024
                    S.op('act', lambda e, sl=sl: e.activation(out=xn[sl][:], in_=xt[sl][:], func=AF.Copy,
                                                              scale=ssn[:, 4 * sl + 2:4 * sl + 3]),
                         reads=['xt%d' % sl, 'ssn%d' % sl], writes=['xn%d' % sl])
                    for k in range(8):
                        b = banks[k // 4]
                        dst = pvb(b).rearrange("p (k t) -> p k t", k=4)[:, k % 4, s * 128:(s + 1) * 128]
                        tp(dst, xn[sl][:, k * 128:(k + 1) * 128], ['xn%d' % sl], [pn(b)])

            def prep_b(n):
                j, T = divmod(n, NTS)
                banks = prepT.pop(n)
                for k in range(8):
                    b = banks[k // 4]
                    src = pvb(b).rearrange("p (k t) -> p k t", k=4)[:, k % 4, :]
                    S.sz = 256
                    S.op('act', lambda e, k=k, src=src: e.activation(out=uTb[n % 2][:, k, :], in_=src, func=AF.Identity,
                                                                     scale=G1[:, k, j:j + 1], bias=S1[:, k, j:j + 1]),
                         reads=[pn(b), 'G1', 'S1'], writes=['uT%d' % (n % 2)])
                ps_rel(banks[0])
                ps_rel(banks[1])

            def afm(n):
                j, T = divmod(n, NTS)
                par = n % 2
                if T == 0:
                    S.op('pool', lambda e: e.memset(conv_halo[:], 0.0), writes=['conv_halo'])
                    S.op('pool', lambda e: e.memset(pool_halo[:], 0.0), writes=['pool_halo'])
                for pair in range(2):
                    b = ps_alloc()
                    bv = pv(b).rearrange("p (a t) -> p a t", a=2)
                    for half in range(2):
                        g = 2 * pair + half
                        for k in range(8):
                            mm(bv[:, half, :], w_in[:, k, g * 128:(g + 1) * 128], uTb[n % 2][:, k, :], k == 0, k == 7,
                               [WIN[k], 'uT%d' % (n % 2)], [pn(b)])
                    sl = nxt('up', 2)
                    un = 'up%d' % sl
                    S.op('pool', lambda e, sl=sl, pair=pair: e.tensor_copy(up[sl][:, :, 0:16], pool_halo[:, 2 * pair:2 * pair + 2, :]),
                         reads=['pool_halo'], writes=[un])
                    S.sz = 512
                    S.op('dve', lambda e, sl=sl, bv=bv: e.tensor_copy(up[sl][:, :, 16:16 + TT], bv),
                         reads=[pn(b)], writes=[un])
                    ps_rel(b)
                    for half in range(2):
                        g = 2 * pair + half
                        eng = 'dve'
                        A, Bt = ta[half], tb[half]
                        An, Bn = 'ta%d' % half, 'tb%d' % half
                        u = up[sl][:, half, :]
                        L = 16 + TT
                        S.sz = 271
                        S.op(eng, lambda e, u=u, A=A: e.tensor_tensor(A[:, 1:L], u[:, 1:L], u[:, 0:L - 1], op=ALU.add),
                             reads=[un], writes=[An])
                        cur, curn = A, An
                        if g >= 1:
                            S.sz = 271
                            S.op(eng, lambda e, A=A, Bt=Bt: e.tensor_tensor(Bt[:, 3:L], A[:, 3:L], A[:, 1:L - 2], op=ALU.add),
                                 reads=[An], writes=[Bn])
                            cur, curn = Bt, Bn
                        if g >= 2:
                            S.sz = 271
                            S.op(eng, lambda e, A=A, Bt=Bt: e.tensor_tensor(A[:, 7:L], Bt[:, 7:L], Bt[:, 3:L - 4], op=ALU.add),
                                 reads=[Bn], writes=[An])
                            cur, curn = A, An
                        if g >= 3:
                            S.sz = 271
                            S.op(eng, lambda e, A=A, Bt=Bt: e.tensor_tensor(Bt[:, 15:L], A[:, 15:L], A[:, 7:L - 8], op=ALU.add),
                                 reads=[An], writes=[Bn])
                            cur, curn = Bt, Bn
                        w = POOL_W[g]
                        S.sz = 256
                        S.op('dve', lambda e, cur=cur, u=u, g=g, w=w: e.scalar_tensor_tensor(
                            out=pbf[:, g, :], in0=cur[:, 16:L], scalar=1.0 / w, in1=u[:, 16:L],
                            op0=ALU.mult, op1=ALU.subtract), reads=[curn, un], writes=['pbf%d' % g])
                        if T == 0:
                            S.op('dve', lambda e, cur=cur, g=g: e.tensor_tensor(
                                t16[:, g, :], cur[:, 16:32], vecs[:, V_INVC + g * 16:V_INVC + g * 16 + 16], op=ALU.mult),
                                 reads=[curn, 'vecs'], writes=['t16_%d' % g])
                            S.op('dve', lambda e, u=u, g=g: e.tensor_tensor(
                                pbf[:, g, 0:16], t16[:, g, :], u[:, 16:32], op=ALU.subtract),
                                 reads=['t16_%d' % g, un, 'pbf%d' % g], writes=['pbf%d' % g])
                    S.op('pool', lambda e, sl=sl, pair=pair: e.tensor_copy(pool_halo[:, 2 * pair:2 * pair + 2, :], up[sl][:, :, TT:TT + 16]),
                         reads=[un], writes=['pool_halo'])
                for pair in range(2):
                    b = ps_alloc()
                    bv = pv(b).rearrange("p (a t) -> p a t", a=2)
                    for half in range(2):
                        g = 2 * pair + half
                        mm(bv[:, half, :], w_pool[:, g, :], pbf[:, g, :], True, True, ['w_pool', 'pbf%d' % g], [pn(b)])
                    for half in range(2):
                        g = 2 * pair + half
                        S.sz = 256
                        S.op('act', lambda e, g=g, half=half, bv=bv: e.activation(
                            out=ypoolT[par][:, g, :], in_=bv[:, half, :], func=AF.Identity,
                            scale=vecs[:, V_PSC + g:V_PSC + g + 1]), reads=[pn(b), 'vecs'], writes=['ypoolT%d' % par])
                    ps_rel(b)

                for pair in range(6):
                    b = ps_alloc()
                    bv = pv(b).rearrange("p (a t) -> p a t", a=2)
                    for half in range(2):
                        e_ = 2 * pair + half
                        c0 = OFF_XBC + e_ * 128
                        for k in range(8):
                            mm(bv[:, half, :], w_in[:, k, c0:c0 + 128], uTb[n % 2][:, k, :], k == 0, k == 7,
                               [WIN[k], 'uT%d' % (n % 2)], [pn(b)])
                    sl = nxt('pre', 3)
                    prn = 'pre%d' % sl
                    S.op('pool', lambda e, sl=sl, pair=pair: e.tensor_copy(pre[sl][:, :, 0:3], conv_halo[:, 2 * pair:2 * pair + 2, :]),
                         reads=['conv_halo'], writes=[prn])
                    S.sz = 512
                    S.op('dve', lambda e, sl=sl, bv=bv: e.tensor_copy(pre[sl][:, :, 3:3 + TT], bv),
                         reads=[pn(b)], writes=[prn])
                    ps_rel(b)
                    b2 = ps_alloc()
                    bv2 = pv(b2).rearrange("p (a t) -> p a t", a=2)
                    for half in range(2):
                        e_ = 2 * pair + half
                        for kk in range(4):
                            mm(bv2[:, half, :], diagC[:, e_ * 4 + kk, :], pre[sl][:, half, kk:kk + TT], kk == 0, kk == 3,
                               ['diagC', prn], [pn(b2)])
                    for half in range(2):
                        e_ = 2 * pair + half
                        S.sz = 256
                        S.op('act', lambda e, half=half, e_=e_, bv2=bv2: e.activation(
                            out=xbcT[par][:, e_, :], in_=bv2[:, half, :], func=AF.Silu, bias=vecs[:, V_CB + e_:V_CB + e_ + 1]),
                             reads=[pn(b2), 'vecs'], writes=['xbcT%d_%d' % (par, e_)])
                    ps_rel(b2)
                    S.op('pool', lambda e, sl=sl, pair=pair: e.tensor_copy(conv_halo[:, 2 * pair:2 * pair + 2, :], pre[sl][:, :, TT:TT + 3]),
                         reads=[prn], writes=['conv_halo'])
            def atm(n):
                par = n % 2
                bdt = ps_alloc()
                dv = pv(bdt)[:, 0:32].rearrange("p (s h) -> p s h", s=2)
                for s in range(2):
                    bz = [ps_alloc(), ps_alloc()]
                    for k in range(8):
                        lt = uTb[n % 2][:, k, s * 128:(s + 1) * 128]
                        for zh in range(2):
                            mm(pv(bz[zh]), lt, w_in[:, k, OFF_Z + zh * 512:OFF_Z + (zh + 1) * 512], k == 0, k == 7,
                               [WIN[k], 'uT%d' % (n % 2)], [pn(bz[zh])])
                        mm(dv[:, s, :], lt, w_in[:, k, OFF_DT:OFF_DT + 16], k == 0, k == 7, [WIN[k], 'uT%d' % (n % 2)], [pn(bdt)])
                    for zh in range(2):
                        S.sz = 512
                        S.op('act', lambda e, s=s, zh=zh, bz=bz: e.activation(
                            out=sz[par][:, s, zh * 512:(zh + 1) * 512], in_=pv(bz[zh]), func=AF.Silu),
                             reads=[pn(bz[zh])], writes=['sz%d' % s])
                        ps_rel(bz[zh])
                S.op('dve', lambda e: e.tensor_tensor(dtb[:], dv, rowv[:, 0:16].unsqueeze(1).broadcast_to([128, 2, 16]), op=ALU.add),
                     reads=[pn(bdt), 'rowv'], writes=['dtb'])
                ps_rel(bdt)
                S.op('act', lambda e: e.activation(out=dtb[:], in_=dtb[:], func=AF.Exp), reads=['dtb'], writes=['dtb'])
                S.op('act', lambda e: e.activation(out=dtv[par][:], in_=dtb[:], func=AF.Ln, bias=1.0),
                     reads=['dtb'], writes=['dtv%d' % par])
                S.op('dve', lambda e: e.tensor_tensor(dabf[par][:], dtv[par][:], a_bc[:].unsqueeze(1).broadcast_to([128, 2, 16]), op=ALU.mult),
                     reads=['dtv%d' % par, 'a_bc'], writes=['dabf%d' % par])

            def bprep(n, c):
                par = n % 2
                cp = c
                tk = slice(c * 128, (c + 1) * 128)
                XB = ['xbcT%d_%d' % (par, e_) for e_ in range(12)]
                b = ps_alloc()
                rhs = dabf[par][:, c, :]
                for i, lt in enumerate((SLm, ones, tri)):
                    mm(pv(b)[:, 16 * i:16 * i + 16], lt, rhs, True, True, ['cst', 'dabf%d' % par], [pn(b)])
                S.op('act', lambda e, b=b: e.activation(out=sm[cp][:], in_=pv(b)[:, 0:48], func=AF.Exp),
                     reads=[pn(b)], writes=['sm%d' % cp])
                ps_rel(b)
                b = ps_alloc()
                sv = pv(b)[:, 0:256].rearrange("p (g l) -> p g l", g=2)
                for g in range(2):
                    mm(sv[:, g, :], xbcT[par][:, 8 + g, tk], xbcT[par][:, 10 + g, tk], True, True,
                       [XB[8 + g], XB[10 + g]], [pn(b)])
                S.sz = 256
                S.op('dve', lambda e, sv=sv: e.tensor_tensor(scm[cp][:], sv, tri.unsqueeze(1).broadcast_to([128, 2, 128]), op=ALU.mult),
                     reads=[pn(b), 'cst'], writes=['scm%d' % cp])
                ps_rel(b)
                for q in range(4):
                    rs = nxt('R', 4)
                    S.sz = 512
                    S.op('pool' if q % 2 == 0 else 'dve', lambda e, rs=rs, q=q: e.tensor_tensor(
                        Rr[rs][:], bc3(dabf[par][:, c, 4 * q:4 * q + 4], 128),
                        tri.unsqueeze(1).broadcast_to([128, 4, 128]), op=ALU.mult),
                         reads=['dabf%d' % par, 'cst'], writes=['Rr%d' % rs])
                    b = ps_alloc()
                    mm(pv(b), SLm, Rr[rs][:].rearrange("p a l -> p (a l)"), True, True, ['cst', 'Rr%d' % rs], [pn(b)])
                    es_ = nxt('E', 3)
                    S.sz = 512
                    S.op('act', lambda e, b=b, es_=es_: e.activation(out=Eq[es_][:].rearrange("p a l -> p (a l)"), in_=pv(b), func=AF.Exp),
                         reads=[pn(b)], writes=['Eq%d' % es_])
                    ps_rel(b)
                    S.sz = 300
                    S.op('dve', lambda e, es_=es_, q=q: e.tensor_tensor(
                        MT[cp][:, 4 * q:4 * q + 4, :], Eq[es_][:], scm[cp][:, q // 2, :].unsqueeze(1).broadcast_to([128, 4, 128]),
                        op=ALU.mult), reads=['Eq%d' % es_, 'scm%d' % cp], writes=['MT%d_%d' % (cp, q)])
                b = ps_alloc()
                for e_ in range(8):
                    tp(pvb(b)[:, e_ * 128:(e_ + 1) * 128], xbcT[par][:, e_, tk], [XB[e_]], [pn(b)])
                xv = pvb(b).rearrange("p (h q) -> p h q", q=64)
                S.sz = 1024
                S.op('dve', lambda e, xv=xv: e.tensor_tensor(xdt[cp][:].rearrange("p (h q) -> p h q", q=64), xv,
                                                            bc3(dtv[par][:, c, :], 64), op=ALU.mult),
                     reads=[pn(b), 'dtv%d' % par], writes=['xdt%d' % cp])
                ps_rel(b)
                S.sz = 1024
                S.op('dve', lambda e: e.tensor_tensor(xdtd[cp][:].rearrange("p (h q) -> p h q", q=64),
                                                       xdt[cp][:].rearrange("p (h q) -> p h q", q=64),
                                                       bc3(sm[cp][:, 0:16], 64), op=ALU.mult),
                     reads=['xdt%d' % cp, 'sm%d' % cp], writes=['xdtd%d' % cp])
                b = ps_alloc()
                for g in range(2):
                    tp(pvb(b)[:, g * 128:(g + 1) * 128], xbcT[par][:, 8 + g, tk], [XB[8 + g]], [pn(b)])
                S.sz = 256
                S.op('act', lambda e, b=b: e.activation(out=Btok[cp][:], in_=pvb(b)[:, 0:256], func=AF.Copy),
                     reads=[pn(b)], writes=['Btok%d' % cp])
                ps_rel(b)

            def bmain(n, c):
                j, T = divmod(n, NTS)
                par = n % 2
                cp = c
                tk = slice(c * 128, (c + 1) * 128)
                XB = ['xbcT%d_%d' % (par, e_) for e_ in range(12)]
                MTN = ['MT%d_%d' % (cp, q) for q in range(4)]
                if T == 0 and c == 0:
                    S.sz = 1024
                    S.op('pool', lambda e: e.memset(Hs[:], 0.0), writes=['Hs0', 'Hs1'])
                    S.sz = 512
                    S.op('pool', lambda e: e.memset(Hbf[:], 0.0), writes=['Hbf0', 'Hbf1'])
                S.op('pool', lambda e: e.memset(ssy[:, 0:2], 0.0), writes=['ssy'])
                for g in range(2):
                    gs = slice(g * 512, (g + 1) * 512)
                    byd = ps_alloc()
                    for ee in range(4):
                        e_ = 4 * g + ee
                        mm(pv(byd)[:, ee * 128:(ee + 1) * 128], xbcT[par][:, e_, tk], diagD[:, e_, :], ee == 0, True,
                           [XB[e_], 'diagD'], [pn(byd)], **({} if ee == 0 else {'skip_group_check': True}))
                    for hh in range(8):
                        h = 8 * g + hh
                        mm(pv(byd)[:, hh * 64:(hh + 1) * 64], MT[cp][:, h, :], xdt[cp][:, h * 64:(h + 1) * 64],
                           False, hh == 7, [MTN[h // 4], 'xdt%d' % cp], [pn(byd)], skip_group_check=True)
                    bst = ps_alloc()
                    mm(pv(bst), Btok[cp][:, g * 128:(g + 1) * 128], xdtd[cp][:, gs], True, True,
                       ['Btok%d' % cp, 'xdtd%d' % cp], [pn(bst)])
                    byo = ps_alloc()
                    mm(pv(byo), xbcT[par][:, 10 + g, tk], Hbf[:, gs], True, True, [XB[10 + g], 'Hbf%d' % g], [pn(byo)])
                    yg = ybuf[:, gs]
                    ygn = 'ybuf%d' % g
                    S.sz = 512
                    S.op('dve', lambda e, byo=byo, yg=yg, g=g: e.tensor_tensor(
                        yg.rearrange("p (h q) -> p h q", q=64), pv(byo).rearrange("p (h q) -> p h q", q=64),
                        bc3(sm[cp][:, 32 + 8 * g:40 + 8 * g], 64), op=ALU.mult),
                         reads=[pn(byo), 'sm%d' % cp], writes=[ygn])
                    ps_rel(byo)
                    S.sz = 512
                    S.op('dve', lambda e, byd=byd, yg=yg: e.tensor_tensor(yg, pv(byd), yg, op=ALU.add),
                         reads=[pn(byd), ygn], writes=[ygn])
                    ps_rel(byd)
                    S.sz = 512
                    S.op('dve', lambda e, yg=yg, gs=gs: e.tensor_tensor(yg, yg, sz[par][:, c, gs], op=ALU.mult),
                         reads=[ygn, 'sz%d' % c], writes=[ygn])
                    S.sz = 512
                    S.op('act', lambda e, yg=yg, gs=gs, g=g: e.activation(out=yn[:, gs], in_=yg, func=AF.Square,
                                                                          accum_out=ssy[:, g:g + 1]),
                         reads=[ygn, 'ssy'], writes=['yn%d' % g, 'ssy'])
                    Hg = Hs[:, gs]
                    S.sz = 512
                    S.op('pool', lambda e, Hg=Hg, g=g: e.tensor_tensor(
                        Hg.rearrange("p (h q) -> p h q", q=64), Hg.rearrange("p (h q) -> p h q", q=64),
                        bc3(sm[cp][:, 16 + 8 * g:24 + 8 * g], 64), op=ALU.mult),
                         reads=['Hs%d' % g, 'sm%d' % cp], writes=['Hs%d' % g])
                    S.sz = 512
                    S.op('dve', lambda e, Hg=Hg, bst=bst, gs=gs: e.tensor_tensor(Hbf[:, gs], pv(bst), Hg, op=ALU.add),
                         reads=[pn(bst), 'Hs%d' % g], writes=['Hbf%d' % g])
                    S.sz = 512
                    S.op('dve', lambda e, Hg=Hg, bst=bst: e.tensor_tensor(Hg, pv(bst), Hg, op=ALU.add),
                         reads=[pn(bst), 'Hs%d' % g], writes=['Hs%d' % g])
                    ps_rel(bst)
                S.op('pool', lambda e: e.tensor_scalar(ssy[:, 2:4], ssy[:, 0:2], 1.0 / 512.0, EPS, op0=ALU.mult, op1=ALU.add),
                     reads=['ssy'], writes=['ssy'])
                S.op('pool', lambda e: e.tensor_tensor(ssy[:, 2:4], ssy[:, 2:4], mhalf[:, 0:2], op=ALU.pow),
                     reads=['ssy', 'mhalf'], writes=['ssy'])
                for g in range(2):
                    gs = slice(g * 512, (g + 1) * 512)
                    S.sz = 512
                    S.op('act', lambda e, gs=gs, g=g: e.activation(out=yn[:, gs], in_=ybuf[:, gs], func=AF.Identity,
                                                                   scale=ssy[:, 2 + g:3 + g]),
                         reads=['ybuf%d' % g, 'ssy'], writes=['yn%d' % g])
                b = ps_alloc()
                tv = pvb(b).rearrange("p (e t) -> p e t", e=8)
                for e_ in range(8):
                    tp(tv[:, e_, :], yn[:, e_ * 128:(e_ + 1) * 128], ['yn%d' % (e_ // 4)], [pn(b)])
                S.sz = 1024
                S.op('dve', lambda e, tv=tv: e.tensor_tensor(yssdT[:, :, tk], tv, bc3(vecs[:, V_GSSD:V_GSSD + 8], 128), op=ALU.mult),
                     reads=[pn(b), 'vecs'], writes=['yssdT%d' % c])
                ps_rel(b)

            def cstage(n, s):
                j, T = divmod(n, NTS)
                par = n % 2
                tk = slice(s * 128, (s + 1) * 128)
                tok0 = T * TT + s * 128
                sl = nxt('x2', 2)
                xn_ = 'x2_%d' % sl
                S.sz = 524288
                S.dma('sp', x2[sl][:], x_d[j, tok0:tok0 + 128, :], writes=[xn_])
                bo = [ps_alloc(), ps_alloc()]
                for e_ in range(12):
                    if e_ < 4:
                        lt, ln = ypoolT[par][:, e_, tk], 'ypoolT%d' % par
                    else:
                        lt, ln = yssdT[:, e_ - 4, tk], 'yssdT%d' % s
                    for dh in range(2):
                        mm(pv(bo[dh]), lt, w_out[:, e_, dh * 512:(dh + 1) * 512], e_ == 0, e_ == 11,
                           [ln, WOUT[e_ // 4]], [pn(bo[dh])] + (['tick%d' % n] if (e_ == 0 and dh == 0 and s == 0) else []))
                for dh in range(2):
                    ds_ = slice(dh * 512, (dh + 1) * 512)
                    S.sz = 512
                    S.op('dve', lambda e, dh=dh, ds_=ds_: e.tensor_tensor(tbuf[dh][:], pv(bo[dh]), gate_m[:, j, ds_], op=ALU.mult),
                         reads=[pn(bo[dh]), 'gate_m'], writes=['tbuf%d' % dh])
                    ps_rel(bo[dh])
                    S.sz = 512
                    S.op('pool', lambda e, dh=dh, ds_=ds_, sl=sl: e.tensor_tensor(x2[sl][:, ds_], tbuf[dh][:], x2[sl][:, ds_], op=ALU.add),
                         reads=['tbuf%d' % dh, xn_], writes=[xn_])
                r0 = j * SEQ + tok0
                S.sz = 524288
                S.dma('sp', h1s_d[r0:r0 + 128, :], x2[sl][:], reads=[xn_], sem='d_h1st%d' % sl)

            def precast(i):
                S.sz = 3145728
                if i < 8:
                    S.dma('pool', wupbf_d[:, :, i, :], wup_d[i * 128:(i + 1) * 128, :].rearrange("p (b c) -> p b c", b=4),
                          sem='d_precast%d' % i, reads=['tick%d' % i], writes=['wbf%d' % i])
                else:
                    k = i - 8
                    S.dma('pool', wdnbf_d[:, 4 * k:4 * k + 4, :], wdown_d[k * 512:(k + 1) * 512, :].rearrange("(f p) d -> p f d", p=128),
                          sem='d_precast%d' % i, reads=['tick%d' % i], writes=['wbf%d' % i])
            prep_a(0)
            prep_b(0)
            afm(0)
            atm(0)
            prep_a(1)
            prep_b(1)
            for n in range(NT):
                bprep(n, 0)
                bprep(n, 1)
                if n + 1 < NT:
                    afm(n + 1)
                bmain(n, 0)
                bmain(n, 1)
                if n + 1 < NT:
                    atm(n + 1)
                if n + 2 < NT:
                    prep_a(n + 2)
                    prep_b(n + 2)
                cstage(n, 0)
                cstage(n, 1)
                precast(n)
            ps_ring[0] = None
            S.barrier()

        with ExitStack() as esF:
            w_up = sbt(esF, "w_up", [128, 4, 8, D], BF16)
            w_down = sbt(esF, "w_down", [128, 32, D], BF16)
            gate_f = sbt(esF, "gate_f", [128, 2, D])
            gfin = sbt(esF, "gfin", [128, D])
            fT = sbt(esF, "fT", [128, 32, TT], BF16)
            crepF = sbt(esF, "crepF", [128, 16, 128], BF16)
            u2T = [sbt(esF, "u2T%d" % i, [128, 8, TT], BF16) for i in range(2)]
            hs = [sbt(esF, "hs%d" % i, [128, D]) for i in range(4)]
            hn = [sbt(esF, "hn%d" % i, [128, D], BF16) for i in range(2)]
            rt = [sbt(esF, "rt%d" % i, [128, 2, TT]) for i in range(2)]
            tbf = [sbt(esF, "tbf%d" % i, [128, 512]) for i in range(2)]
            ssf = sbt(esF, "ssf", [128, 32])

            WUP = ['w_up%d' % k for k in range(4)]
            fcnt = {'hs': 0, 'rt': 0, 'hn': 0}
            fprepT = {}
            hslot = {}

            def fnxt(key, n_):
                v = fcnt[key] % n_
                fcnt[key] += 1
                return v

            def fprep(m):
                j, T = divmod(m, NTS)
                banks = [ps_alloc(), ps_alloc()]
                hslot[m] = []
                for s in range(2):
                    sl = fnxt('hs', 4)
                    hl = fnxt('hn', 2)
                    hslot[m].append(sl)
                    r0 = j * SEQ + T * TT + s * 128
                    c0 = 4 * sl
                    S.sz = 524288
                    S.dma('sp', hs[sl][:], h1s_d[r0:r0 + 128, :], writes=['hs%d' % sl])
                    S.op('pool', lambda e, c0=c0: e.memset(ssf[:, c0:c0 + 1], 0.0), writes=['ssf%d' % sl])
                    S.sz = 1024
                    S.op('act', lambda e, sl=sl, hl=hl, c0=c0: e.activation(out=hn[hl][:], in_=hs[sl][:], func=AF.Square,
                                                                            accum_out=ssf[:, c0:c0 + 1]),
                         reads=['hs%d' % sl, 'ssf%d' % sl], writes=['hn%d' % hl, 'ssf%d' % sl])
                    S.op('pool', lambda e, c0=c0: e.tensor_scalar(ssf[:, c0 + 1:c0 + 2], ssf[:, c0:c0 + 1], 1024.0 * EPS, None, op0=ALU.add),
                         reads=['ssf%d' % sl], writes=['ssf%d' % sl])
                    S.op('pool', lambda e, c0=c0: e.tensor_tensor(ssf[:, c0 + 2:c0 + 3], ssf[:, c0 + 1:c0 + 2], mhalf[:, 0:1], op=ALU.pow),
                         reads=['ssf%d' % sl, 'mhalf'], writes=['ssf%d' % sl])
                    S.sz = 1024
                    S.op('act', lambda e, sl=sl, hl=hl, c0=c0: e.activation(out=hn[hl][:], in_=hs[sl][:], func=AF.Copy, scale=ssf[:, c0 + 2:c0 + 3]),
                         reads=['hs%d' % sl, 'ssf%d' % sl], writes=['hn%d' % hl])
                    for k in range(8):
                        b = banks[k // 4]
                        dst = pvb(b).rearrange("p (k t) -> p k t", k=4)[:, k % 4, s * 128:(s + 1) * 128]
                        tp(dst, hn[hl][:, k * 128:(k + 1) * 128], ['hn%d' % hl], [pn(b)])
                up_ = m % 2
                for k in range(8):
                    b = banks[k // 4]
                    src = pvb(b).rearrange("p (k t) -> p k t", k=4)[:, k % 4, :]
                    S.sz = 256
                    S.op('act', lambda e, k=k, src=src: e.activation(out=u2T[up_][:, k, :], in_=src, func=AF.Identity,
                                                                     scale=G2[:, k, j:j + 1], bias=S2[:, k, j:j + 1]),
                         reads=[pn(b), 'G2', 'S2'], writes=['u2T%d' % up_])
                ps_rel(banks[0])
                ps_rel(banks[1])

            def fup(m):
                up_ = m % 2
                for fp in range(16):
                    b = ps_alloc()
                    bv = pv(b).rearrange("p (a t) -> p a t", a=2)
                    for half in range(2):
                        f = 2 * fp + half
                        for k in range(8):
                            mm(bv[:, half, :], w_up[:, f // 8, k, (f % 8) * 128:(f % 8 + 1) * 128], u2T[up_][:, k, :], k == 0, k == 7,
                               [WUP[f // 8], 'u2T%d' % up_], [pn(b)])
                    rl = fnxt('rt', 2)
                    S.sz = 512
                    S.op('act', lambda e, rl=rl, bv=bv: e.activation(out=rt[rl][:], in_=bv, func=AF.Relu),
                         reads=[pn(b)], writes=['rt%d' % rl])
                    ps_rel(b)
                    eng = 'pool' if fp % 3 == 2 else 'dve'
                    S.sz = 512
                    S.op(eng, lambda e, rl=rl, fp=fp: e.tensor_tensor(fT[:, 2 * fp:2 * fp + 2, :], rt[rl][:], rt[rl][:], op=ALU.mult),
                         reads=['rt%d' % rl], writes=['fT'])

            def fdown(m, s):
                j, T = divmod(m, NTS)
                tk = slice(s * 128, (s + 1) * 128)
                sl = hslot[m][s]
                hn_ = 'hs%d' % sl
                c0 = 4 * sl
                bd = [ps_alloc(), ps_alloc()]
                for f in range(32):
                    for dh in range(2):
                        mm(pv(bd[dh]), fT[:, f, tk], w_down[:, f, dh * 512:(dh + 1) * 512], f == 0, f == 31,
                           ['fT', 'w_down%d' % (f // 8)], [pn(bd[dh])])
                for dh in range(2):
                    ds_ = slice(dh * 512, (dh + 1) * 512)
                    S.sz = 512
                    S.op('dve', lambda e, dh=dh, ds_=ds_: e.tensor_tensor(tbf[dh][:], pv(bd[dh]), gate_f[:, j, ds_], op=ALU.mult),
                         reads=[pn(bd[dh]), 'gate_f'], writes=['tbf%d' % dh])
                    ps_rel(bd[dh])
                    S.sz = 512
                    S.op('pool', lambda e, dh=dh, ds_=ds_: e.tensor_tensor(hs[sl][:, ds_], tbf[dh][:], hs[sl][:, ds_], op=ALU.add),
                         reads=['tbf%d' % dh, hn_], writes=[hn_])
                hl = fnxt('hn', 2)
                S.op('pool', lambda e: e.memset(ssf[:, c0:c0 + 1], 0.0), writes=['ssf%d' % sl])
                S.sz = 1024
                S.op('act', lambda e: e.activation(out=hn[hl][:], in_=hs[sl][:], func=AF.Square, accum_out=ssf[:, c0:c0 + 1]),
                     reads=[hn_, 'ssf%d' % sl], writes=['hn%d' % hl, 'ssf%d' % sl])
                S.op('pool', lambda e: e.tensor_scalar(ssf[:, c0 + 1:c0 + 2], ssf[:, c0:c0 + 1], 1.0 / 1024.0, EPS, op0=ALU.mult, op1=ALU.add),
                     reads=['ssf%d' % sl], writes=['ssf%d' % sl])
                S.op('pool', lambda e: e.tensor_tensor(ssf[:, c0 + 2:c0 + 3], ssf[:, c0 + 1:c0 + 2], mhalf[:, 0:1], op=ALU.pow),
                     reads=['ssf%d' % sl, 'mhalf'], writes=['ssf%d' % sl])
                S.sz = 1024
                S.op('dve', lambda e: e.scalar_tensor_tensor(out=hs[sl][:], in0=hs[sl][:], scalar=ssf[:, c0 + 2:c0 + 3], in1=gfin[:],
                                                             op0=ALU.mult, op1=ALU.mult),
                     reads=[hn_, 'ssf%d' % sl, 'gfin'], writes=[hn_])
                tok0 = T * TT + s * 128
                S.sz = 524288
                S.dma('sp', out_d[j, tok0:tok0 + 128, :], hs[sl][:], reads=[hn_], sem='d_out%d' % sl)

            fprep(0)
            for fb in range(4):
                S.sz = 2097152
                S.dma('sp', w_up[:, fb, :, :], wupbf_d[:, fb, :, :], writes=['w_up%d' % fb])
            make_crep(crepF)
            for j in range(2):
                S.sz = 524288
                S.dma('sp', gate_f[:, j, :], gbias_d[:, 1024:2048], writes=['gate_f'], sem='d_gf%d' % j)
            S.sz = 524288
            S.dma('sp', gfin[:], gfin_d, writes=['gfin'])
            wadaF = fT[:].rearrange("p f t -> p (f t)").rearrange("p (k c) -> p k c", k=8)
            ada_block(5, wadaF, 'fT', crepF, gate=(gate_f, 'gate_f'))
            for i in range(4):
                S.sz = 2097152
                S.dma('sp', w_down[:, 8 * i:8 * i + 8, :], wdnbf_d[:, 8 * i:8 * i + 8, :], writes=['w_down%d' % i])
            for m in range(NT):
                fup(m)
                if m + 1 < NT:
                    fprep(m + 1)
                fdown(m, 0)
                fdown(m, 1)
            S.barrier()
    return nc


_NC_CACHE = {}


def _consts():
    i = np.arange(128)
    ident = np.eye(128, dtype=np.float32)
    tri = (i[:, None] <= i[None, :]).astype(np.float32)
    sl = (i[:, None] > i[None, :]).astype(np.float32)
    ones = np.ones((128, 128), np.float32)
    return np.ascontiguousarray(np.concatenate([ident, tri, sl, ones], axis=1))


def _fm(v, nchunk):
    return np.ascontiguousarray(np.asarray(v, np.float32).reshape(nchunk, 128).T)


def make_in_maps(x, c, w_ada, b_ada, g_mix, w_in, conv_w, conv_b, dt_bias, a_log, d_skip, g_ssd,
                 w_pool, pool_scale, w_out, g_mlp, w_up, w_down, g_final):
    f = lambda a: np.ascontiguousarray(np.asarray(a, dtype=np.float32))
    x = f(x); c = f(c)
    w_ada0 = f(w_ada[0]); b0 = f(b_ada[0])
    consts = _consts()
    invc = np.zeros((128, 4, 16), np.float32)
    for g, w in enumerate(POOL_W):
        invc[:, g, :] = 1.0 / np.minimum(np.arange(1, 17), w)
    cw = f(conv_w[0])
    cw_fm = np.ascontiguousarray(cw.T.reshape(12, 128, 4).transpose(1, 0, 2)).reshape(128, 48)
    gbias = np.ascontiguousarray(np.broadcast_to(np.concatenate([b0[2048:3072], b0[5120:6144]])[None, :], (128, 2048)))
    gfin = np.ascontiguousarray(np.broadcast_to(f(g_final)[None, :], (128, D)))
    rowv = np.ascontiguousarray(np.broadcast_to(
        np.concatenate([f(dt_bias[0]), f(a_log[0]), f(d_skip[0])])[None, :], (128, 48)))
    shared = dict(consts=consts, gbias=gbias, gfin=gfin, rowv=rowv, w_ada=w_ada0, w_in=f(w_in[0]),
                  w_out=f(w_out[0]), w_up=f(w_up[0]), w_down=f(w_down[0]),
                  w_pool=f(w_pool[0]).reshape(512, 128))
    in_maps = []
    for core in range(NCORES):
        vecs = np.zeros((128, NV), np.float32)
        vecs[:, V_GMIX:V_GMIX + 8] = _fm(g_mix[0], 8)
        vecs[:, V_GMLP:V_GMLP + 8] = _fm(g_mlp[0], 8)
        vecs[:, V_CW:V_CW + 48] = cw_fm
        vecs[:, V_CB:V_CB + 12] = _fm(conv_b[0], 12)
        vecs[:, V_PSC:V_PSC + 4] = _fm(pool_scale[0], 4)
        vecs[:, V_GSSD:V_GSSD + 8] = _fm(g_ssd[0], 8)
        vecs[:, V_BT:V_BT + 48] = _fm(b0, 48)
        vecs[:, V_INVC:V_INVC + 64] = invc.reshape(128, 64)
        vecs[:, V_DSK:V_DSK + 8] = _fm(np.repeat(f(d_skip[0]), 64), 8)
        cc = c[NB * core:NB * core + NB]
        vecs[:, V_CT:V_CT + 16] = np.ascontiguousarray(cc.reshape(NB, 8, 128).transpose(2, 1, 0)).reshape(128, 16)
        m = dict(shared)
        m["x"] = np.ascontiguousarray(x[NB * core:NB * core + NB])
        m["vecs"] = vecs
        in_maps.append(m)
    return in_maps


def kernel(**inputs):
    if "nc" not in _NC_CACHE:
        _NC_CACHE["nc"] = build_nc()
    nc = _NC_CACHE["nc"]
    in_maps = make_in_maps(**inputs)
    res = run_bass_kernel_spmd(nc, in_maps, core_ids=list(range(NCORES)))
    out = np.concatenate([np.asarray(r["out"]) for r in res.results], axis=0)
    return out.astype(np.float32)
```

```python
import types
import numpy as np
from contextlib import ExitStack
import concourse.bass as bass
import concourse.mybir as mybir
from concourse.bass_utils import run_bass_kernel_spmd

F32 = mybir.dt.float32
BF16 = mybir.dt.bfloat16
AF = mybir.ActivationFunctionType
ALU = mybir.AluOpType

NCORES = 8
D = 1024
SEQ = 2048
NB = 2
TT = 256
NTS = SEQ // TT
NT = NB * NTS
EPS = 1e-5
IN_W = 3088
OFF_Z = 512
OFF_XBC = 1536
OFF_DT = 3072
POOL_W = (2, 4, 8, 16)

V_GMIX, V_GMLP, V_CW, V_CB, V_PSC, V_GSSD, V_BT, V_DTB, V_ALOG, V_DSK, V_INVC, V_CT = (
    0, 8, 16, 64, 76, 80, 88, 136, 152, 168, 184, 248)
NV = 264


def _freeze(fn):
    if fn.__closure__ is None:
        return fn
    cells = tuple(types.CellType(c.cell_contents) for c in fn.__closure__)
    return types.FunctionType(fn.__code__, fn.__globals__, fn.__name__, fn.__defaults__, cells)


def _cost(e, n):
    if e == 'pe':
        return 6.0 + 0.45 * n + (0.07 * (n - 256) if n > 256 else 0.0)
    if e == 'act':
        return 40.0 + (250.0 + n) / 1.5
    if e == 'dve':
        return (180.0 + n) / 0.96
    if e == 'pool':
        return 200.0 + 1.8 * n
    return 60.0


class Sched:
    ENG = ('pe', 'act', 'dve', 'pool', 'sp')
    DMA_BW = 280.0
    DMA_LAT = 2000.0
    XLAT = 600.0
    SLAT = 60.0

    def __init__(self, nc, es):
        self.nc = nc
        self.es = es
        self.eng = {'pe': nc.tensor, 'act': nc.scalar, 'dve': nc.vector,
                    'pool': nc.gpsimd, 'sp': nc.sync}
        self.sem = {e: es.enter_context(nc.semaphore("c_" + e)) for e in self.ENG}
        self.cnt = {e: 0 for e in self.ENG}
        self.seen = {e: {} for e in self.ENG}
        self.dma_sems = {}
        self.res = {}
        self.ops = []
        self.sim_total = 0.0
        self.sz = None

    def _r(self, name):
        st = self.res.get(name)
        if st is None:
            st = {'w': None, 'r': []}
            self.res[name] = st
        return st

    def _record(self, rec, reads, writes):
        ops = self.ops
        e = rec['e']
        is_dma = rec['dma']
        deps = {}

        def add(d, raw):
            o = ops[d]
            need = is_dma or o['dma'] or o['e'] != e or e != 'pe'
            deps[d] = deps.get(d, False) or need

        for r in reads:
            w = self._r(r)['w']
            if w is not None:
                add(w, True)
        for wn in writes:
            st = self._r(wn)
            if st['w'] is not None:
                add(st['w'], False)
            for d in st['r']:
                add(d, False)
        rec['deps'] = deps
        rec['id'] = len(ops)
        ops.append(rec)
        for r in reads:
            self._r(r)['r'].append(rec['id'])
        for wn in writes:
            st = self._r(wn)
            st['w'] = rec['id']
            st['r'] = []

    def op(self, e, fn, reads=(), writes=(), n=None):
        if n is None:
            n = self.sz if self.sz is not None else 64
        self.sz = None
        self._record({'e': e, 'dma': False, 'fn': _freeze(fn), 'cost': _cost(e, n)}, reads, writes)

    def dma(self, q, out, in_, reads=(), writes=(), sem=None, nbytes=None):
        if nbytes is None:
            nbytes = self.sz if self.sz is not None else 65536
        self.sz = None
        name = sem or ('d_' + (writes[0] if writes else reads[0]))
        if name not in self.dma_sems:
            self.dma_sems[name] = [self.es.enter_context(self.nc.semaphore(name)), 0]
        self._record({'e': q, 'dma': True, 'out': out, 'in_': in_, 'sem': name, 'cost': 80.0,
                      'nbytes': nbytes}, reads, writes)

    INORDER_SEGS = (0,)
    seg_idx = 0

    def _schedule(self):
        ops = self.ops
        nops = len(ops)
        seg = self.seg_idx
        self.seg_idx += 1
        if seg in self.INORDER_SEGS:
            order = {e: [] for e in self.ENG}
            for o in ops:
                order[o['e']].append(o['id'])
            return order
        succ = [[] for _ in range(nops)]
        nun = [0] * nops
        for o in ops:
            nun[o['id']] = len(o['deps'])
            for d in o['deps']:
                succ[d].append(o['id'])
        ready = {e: [] for e in self.ENG}
        rtime = [0.0] * nops
        fin = [0.0] * nops
        for o in ops:
            if nun[o['id']] == 0:
                ready[o['e']].append(o['id'])
        efree = {e: 0.0 for e in self.ENG}
        dma_free = 0.0
        order = {e: [] for e in self.ENG}
        done = 0
        while done < nops:
            best = None
            for e in self.ENG:
                rl = ready[e]
                if not rl:
                    continue
                tmin = min(rtime[i] for i in rl)
                t = max(efree[e], tmin)
                if best is None or t < best[0]:
                    best = (t, e)
            t, e = best
            rl = ready[e]
            cand = min(i for i in rl if rtime[i] <= t)
            rl.remove(cand)
            o = ops[cand]
            start = t
            efree[e] = start + o['cost']
            if o['dma']:
                st = max(efree[e], dma_free)
                dma_free = st + o['nbytes'] / self.DMA_BW
                fin[cand] = dma_free + self.DMA_LAT
            else:
                fin[cand] = efree[e]
            order[e].append(cand)
            done += 1
            for s_ in succ[cand]:
                nun[s_] -= 1
                lat = fin[cand] + (self.XLAT if (ops[s_]['e'] != e or o['dma']) else self.SLAT)
                if lat > rtime[s_]:
                    rtime[s_] = lat
                if nun[s_] == 0:
                    ready[ops[s_]['e']].append(s_)
        self.sim_total += max(fin) if nops else 0.0
        return order

    def flush(self):
        ops = self.ops
        if not ops:
            return
        order = self._schedule()
        pos = [0] * len(ops)
        for e in self.ENG:
            for p_, i in enumerate(order[e]):
                pos[i] = p_
        waited = [False] * len(ops)
        sel = [None] * len(ops)
        for o in ops:
            best = {}
            lst = []
            for d, w in o['deps'].items():
                if not w:
                    continue
                od = ops[d]
                if od['dma']:
                    lst.append(d)
                else:
                    k = od['e']
                    if k not in best or pos[best[k]] < pos[d]:
                        best[k] = d
            lst.extend(best.values())
            sel[o['id']] = lst
            for d in lst:
                waited[d] = True
        for e in self.ENG:
            nd = [i for i in order[e] if not ops[i]['dma']]
            if nd:
                waited[nd[-1]] = True
        tok = [None] * len(ops)
        for e in self.ENG:
            c = self.cnt[e]
            dc = {}
            for i in order[e]:
                o = ops[i]
                if o['dma']:
                    rec = self.dma_sems[o['sem']]
                    v = dc.get(o['sem'], rec[1]) + 16
                    dc[o['sem']] = v
                    tok[i] = (o['sem'], v)
                elif waited[i]:
                    c += 1
                    tok[i] = (e, c)
        for e in self.ENG:
            eng = self.eng[e]
            seen = self.seen[e]
            for i in order[e]:
                o = ops[i]
                need = {}
                for d in sel[i]:
                    k, v = tok[d]
                    if need.get(k, 0) < v:
                        need[k] = v
                for k, v in need.items():
                    if seen.get(k, 0) < v:
                        eng.wait_ge(self.sem[k] if k in self.sem else self.dma_sems[k][0], v)
                        seen[k] = v
                if o['dma']:
                    ins = eng.dma_start(out=o['out'], in_=o['in_'])
                    rec = self.dma_sems[o['sem']]
                    rec[1] += 16
                    assert rec[1] == tok[i][1]
                    ins.then_inc(rec[0], 16)
                else:
                    ins = o['fn'](eng)
                    if waited[i]:
                        self.cnt[e] += 1
                        assert self.cnt[e] == tok[i][1]
                        ins.then_inc(self.sem[e], 1)
        self.ops = []
        self.res = {}

    def wait_all(self, e):
        for k in self.ENG:
            if k != e and self.cnt[k] > self.seen[e].get(k, 0):
                self.eng[e].wait_ge(self.sem[k], self.cnt[k])
                self.seen[e][k] = self.cnt[k]
        for k, (s, v) in self.dma_sems.items():
            if v > self.seen[e].get(k, 0):
                self.eng[e].wait_ge(s, v)
                self.seen[e][k] = v

    def barrier(self):
        self.flush()
        for e in self.ENG:
            self.wait_all(e)


def build_nc(debug=False):
    nc = bass.Bass("TRN2", target_bir_lowering=False)

    def din(name, shape):
        return nc.dram_tensor(name, shape, F32, kind="ExternalInput").ap()

    x_d = din("x", [NB, SEQ, D])
    vecs_d = din("vecs", [128, NV])
    consts_d = din("consts", [128, 512])
    gbias_d = din("gbias", [128, 2048])
    gfin_d = din("gfin", [128, D])
    rowv_d = din("rowv", [128, 48])
    wada_d = din("w_ada", [D, 6 * D])
    win_d = din("w_in", [D, IN_W])
    wout_d = din("w_out", [1536, D])
    wup_d = din("w_up", [D, 4 * D])
    wdown_d = din("w_down", [4 * D, D])
    wpool_d = din("w_pool", [512, 128])
    out_d = nc.dram_tensor("out", [NB, SEQ, D], F32, kind="ExternalOutput").ap()
    wupbf_d = nc.dram_tensor("wup_bf", [128, 4, 8, D], BF16, kind="Internal").ap()
    wdnbf_d = nc.dram_tensor("wdn_bf", [128, 32, D], BF16, kind="Internal").ap()
    h1s_d = nc.dram_tensor("h1s", [NB * SEQ, D], F32,
                           kind="ExternalOutput" if debug else "Internal").ap()

    with ExitStack() as es0:
        S = Sched(nc, es0)

        def sbt(es, name, shape, dt=F32):
            return es.enter_context(nc.sbuf_tensor("s_" + name, shape, dt))

        psb = [es0.enter_context(nc.psum_tensor("psb%d" % i, [128, 512], F32)) for i in range(8)]
        ps_free = [True] * 8
        ps_next = [0]

        ps_ring = [None]
        ps_nx = {}

        def ps_alloc(ring=None):
            lo, hi = ring or ps_ring[0] or (0, 8)
            n_ = hi - lo
            st = ps_nx.get((lo, hi), 0)
            for t in range(n_):
                i = lo + (st + t) % n_
                if ps_free[i]:
                    ps_free[i] = False
                    ps_nx[(lo, hi)] = (i - lo + 1) % n_
                    return i
            raise RuntimeError("no free PSUM bank")

        def ps_rel(i):
            ps_free[i] = True

        def pv(i):
            return psb[i][:]

        def pvb(i):
            return psb[i][:].bitcast(BF16)

        def pn(i):
            return "ps%d" % i

        vecs = sbt(es0, "vecs", [128, NV])
        rowv = sbt(es0, "rowv", [128, 48])
        cst = sbt(es0, "cst", [128, 512], BF16)
        ident = cst[:, 0:128]
        tri = cst[:, 128:256]
        SLm = cst[:, 256:384]
        ones = cst[:, 384:512]
        cact = sbt(es0, "cact", [128, 16], BF16)
        G1 = sbt(es0, "G1", [128, 8, 2])
        S1 = sbt(es0, "S1", [128, 8, 2])
        G2 = sbt(es0, "G2", [128, 8, 2])
        S2 = sbt(es0, "S2", [128, 8, 2])
        gm32 = sbt(es0, "gm32", [128, 16])
        a_bc = sbt(es0, "a_bc", [128, 16])
        mhalf = sbt(es0, "mhalf", [128, 4])
        smallt = sbt(es0, "smallt", [128, 16])

        S.dma('sp', vecs[:], vecs_d, writes=['vecs'])
        S.dma('sp', rowv[:], rowv_d, writes=['rowv'])
        S.sz = 262144
        S.dma('pool', cst[:], consts_d, writes=['cst'])
        S.op('pool', lambda e: e.memset(mhalf[:], -0.5), writes=['mhalf'])

        def mm(out, lhsT, rhs, start, stop, R, W, **kw):
            S.sz = int(np.prod(rhs.shape[1:]))
            S.op('pe', lambda e: e.matmul(out, lhsT, rhs, start=start, stop=stop, **kw),
                 reads=R, writes=W)

        def tp(out, in_, R, W):
            S.sz = 128
            S.op('pe', lambda e: e.transpose(out, in_, ident), reads=R + ['cst'], writes=W)

        S.op('act', lambda e: e.activation(out=cact[:], in_=vecs[:, V_CT:V_CT + 16], func=AF.Silu),
             reads=['vecs'], writes=['cact'])
        S.op('act', lambda e: e.activation(out=a_bc[:], in_=rowv[:, 16:32], func=AF.Exp),
             reads=['rowv'], writes=['a_bc'])
        S.op('dve', lambda e: e.tensor_scalar(a_bc[:], a_bc[:], -1.0, None, op0=ALU.mult),
             reads=['a_bc'], writes=['a_bc'])
        S.op('dve', lambda e: e.tensor_scalar(gm32[:], vecs[:, V_GMIX:V_GMIX + 16], 32.0, None, op0=ALU.mult),
             reads=['vecs'], writes=['gm32'])

        def bc3(ap2, n):
            return ap2.unsqueeze(2).broadcast_to([ap2.shape[0], ap2.shape[1], n])

        def ada_block(blk, wbuf, wname, crep, gate=None):
            S.sz = 4194304
            S.dma('pool', wbuf[:], wada_d[:, blk * 1024:(blk + 1) * 1024].rearrange("(k p) c -> p k c", p=128),
                  writes=[wname])
            if gate is None:
                b = ps_alloc()
                pm = pv(b)[:, 0:16].rearrange("p (e j) -> p e j", j=2)
                for e_ in range(8):
                    for k in range(8):
                        mm(pm[:, e_, :], wbuf[:, k, e_ * 128:(e_ + 1) * 128], cact[:, 2 * k:2 * k + 2],
                           k == 0, k == 7, [wname, 'cact'], [pn(b)])
                bt = bc3(vecs[:, V_BT + blk * 8:V_BT + blk * 8 + 8], 2)
                dst, dn = {0: (S1, 'S1'), 1: (G1, 'G1'), 3: (S2, 'S2'), 4: (G2, 'G2')}[blk]
                S.sz = 16
                S.op('dve', lambda e: e.tensor_tensor(dst[:], pm, bt, op=ALU.add),
                     reads=[pn(b), 'vecs'], writes=[dn])
                ps_rel(b)
                if blk in (1, 4):
                    gsel = gm32[:, 0:8] if blk == 1 else gm32[:, 8:16]
                    S.sz = 16
                    S.op('dve', lambda e: e.scalar_tensor_tensor(out=dst[:], in0=dst[:], scalar=1.0,
                                                                 in1=bc3(gsel, 2), op0=ALU.add, op1=ALU.mult),
                         reads=[dn, 'gm32'], writes=[dn])
            else:
                gbuf, gname = gate
                for j in range(2):
                    for dh in range(2):
                        b = ps_alloc()
                        for k in range(8):
                            mm(pv(b), crep[:, 2 * k + j, :], wbuf[:, k, dh * 512:(dh + 1) * 512],
                               k == 0, k == 7, [wname, 'crep'], [pn(b)])
                        gs = gbuf[:, j, dh * 512:(dh + 1) * 512]
                        S.sz = 512
                        S.op('dve', lambda e, gs=gs, b=b: e.tensor_tensor(gs, pv(b), gs, op=ALU.add),
                             reads=[pn(b), gname], writes=[gname])
                        ps_rel(b)

        def make_crep(crep):
            S.sz = 2048
            S.op('dve', lambda e: e.tensor_copy(crep[:], bc3(cact[:], 128)), reads=['cact'], writes=['crep'])

        with ExitStack() as esM:
            w_in = sbt(esM, "w_in", [128, 8, IN_W], BF16)
            w_out = sbt(esM, "w_out", [128, 12, D], BF16)
            w_pool = sbt(esM, "w_pool", [128, 4, 128], BF16)
            gate_m = sbt(esM, "gate_m", [128, 2, D])
            diagC = sbt(esM, "diagC", [128, 48, 128], BF16)
            diagD = sbt(esM, "diagD", [128, 8, 128], BF16)
            S.sz = 6144
            S.op('dve', lambda e: e.tensor_tensor(diagC[:], ident.unsqueeze(1).broadcast_to([128, 48, 128]),
                                                  bc3(vecs[:, V_CW:V_CW + 48], 128), op=ALU.mult),
                 reads=['cst', 'vecs'], writes=['diagC'])
            S.sz = 1024
            S.op('dve', lambda e: e.tensor_tensor(diagD[:], ident.unsqueeze(1).broadcast_to([128, 8, 128]),
                                                  bc3(vecs[:, V_DSK:V_DSK + 8], 128), op=ALU.mult),
                 reads=['cst', 'vecs'], writes=['diagD'])

            with ExitStack() as esP:
                wada = [sbt(esP, "wada%d" % i, [128, 8, 1024], BF16) for i in range(2)]
                crep = sbt(esP, "crep", [128, 16, 128], BF16)
                make_crep(crep)
                for j in range(2):
                    S.sz = 524288
                    S.dma('sp', gate_m[:, j, :], gbias_d[:, 0:1024], writes=['gate_m'], sem='d_gm%d' % j)
                ada_block(0, wada[0], 'wada0', crep)
                ada_block(1, wada[1], 'wada1', crep)
                for k in range(8):
                    S.sz = 1581056
                    S.dma('pool', w_in[:, k, :], win_d[k * 128:(k + 1) * 128, :], writes=['w_in%d' % k])
                S.sz = 262144
                S.dma('pool', w_pool[:], wpool_d.rearrange("(g p) d -> p g d", p=128), writes=['w_pool'])
                ada_block(2, wada[0], 'wada0', crep, gate=(gate_m, 'gate_m'))
                for i in range(3):
                    S.sz = 2097152
                    S.dma('pool', w_out[:, 4 * i:4 * i + 4, :],
                          wout_d[i * 512:(i + 1) * 512, :].rearrange("(e p) d -> p e d", p=128),
                          writes=['w_out%d' % i])
                ada_block(3, wada[1], 'wada1', crep)
                ada_block(4, wada[0], 'wada0', crep)
                S.barrier()
            WIN = ['w_in%d' % k for k in range(8)]
            WOUT = ['w_out%d' % i for i in range(3)]

            xt = [sbt(esM, "xt%d" % i, [128, D]) for i in range(2)]
            xn = [sbt(esM, "xn%d" % i, [128, D], BF16) for i in range(2)]
            ssn = sbt(esM, "ssn", [128, 8])
            uTb = [sbt(esM, "uT%d" % i, [128, 8, TT], BF16) for i in range(2)]
            up = [sbt(esM, "up%d" % i, [128, 2, 16 + TT]) for i in range(2)]
            ta = [sbt(esM, "ta%d" % i, [128, 16 + TT]) for i in range(2)]
            tb = [sbt(esM, "tb%d" % i, [128, 16 + TT]) for i in range(2)]
            t16 = sbt(esM, "t16", [128, 4, 16])
            pbf = sbt(esM, "pbf", [128, 4, TT], BF16)
            pool_halo = sbt(esM, "pool_halo", [128, 4, 16])
            pre = [sbt(esM, "pre%d" % i, [128, 2, 4 + TT], BF16) for i in range(3)]
            conv_halo = sbt(esM, "conv_halo", [128, 12, 3], BF16)
            xbcT = [sbt(esM, "xbcT%d" % i, [128, 12, TT], BF16) for i in range(2)]
            sz = [sbt(esM, "sz%d" % i, [128, 2, D], BF16) for i in range(1)] * 2
            dtb = sbt(esM, "dtb", [128, 2, 16])
            dtv = [sbt(esM, "dtv%d" % i, [128, 2, 16]) for i in range(2)]
            dabf = [sbt(esM, "dabf%d" % i, [128, 2, 16], BF16) for i in range(2)]
            Rr = [sbt(esM, "Rr%d" % i, [128, 4, 128], BF16) for i in range(4)]
            Eq = [sbt(esM, "Eq%d" % i, [128, 4, 128], BF16) for i in range(3)]
            MT = [sbt(esM, "MT%d" % i, [128, 16, 128], BF16) for i in range(2)]
            scm = [sbt(esM, "scm%d" % i, [128, 2, 128], BF16) for i in range(2)]
            sm = [sbt(esM, "sm%d" % i, [128, 48]) for i in range(2)]
            xdt = [sbt(esM, "xdt%d" % i, [128, D], BF16) for i in range(2)]
            xdtd = [sbt(esM, "xdtd%d" % i, [128, D], BF16) for i in range(2)]
            Btok = [sbt(esM, "Btok%d" % i, [128, 256], BF16) for i in range(2)]
            Hs = sbt(esM, "Hs", [128, D])
            Hbf = sbt(esM, "Hbf", [128, D], BF16)
            ybuf = sbt(esM, "ybuf", [128, D])
            yn = sbt(esM, "yn", [128, D], BF16)
            ssy = sbt(esM, "ssy", [128, 4])
            ypoolT = [sbt(esM, "ypoolT%d" % i, [128, 4, TT], BF16) for i in range(2)]
            yssdT = sbt(esM, "yssdT", [128, 8, TT], BF16)
            tbuf = [sbt(esM, "tbuf%d" % i, [128, 512]) for i in range(2)]
            x2 = [sbt(esM, "x2_%d" % i, [128, D]) for i in range(2)]
            cnt = {'xt': 0, 'pre': 0, 'acc': 0, 'up': 0, 'R': 0, 'E': 0, 'x2': 0}

            def nxt(key, n):
                v = cnt[key] % n
                cnt[key] += 1
                return v

            prepT = {}

            def prep_a(n):
                j, T = divmod(n, NTS)
                banks = [ps_alloc(), ps_alloc()]
                prepT[n] = banks
                for s in range(2):
                    sl = nxt('xt', 2)
                    tok0 = T * TT + s * 128
                    S.sz = 524288
                    S.dma('sp', xt[sl][:], x_d[j, tok0:tok0 + 128, :], writes=['xt%d' % sl])
                    S.op('pool', lambda e, sl=sl: e.memset(ssn[:, 4 * sl:4 * sl + 1], 0.0), writes=['ssn%d' % sl])
                    S.sz = 1024
                    S.op('act', lambda e, sl=sl: e.activation(out=xn[sl][:], in_=xt[sl][:], func=AF.Square,
                                                              accum_out=ssn[:, 4 * sl:4 * sl + 1]),
                         reads=['xt%d' % sl, 'ssn%d' % sl], writes=['xn%d' % sl, 'ssn%d' % sl])
                    S.op('pool', lambda e, sl=sl: e.tensor_scalar(ssn[:, 4 * sl + 1:4 * sl + 2], ssn[:, 4 * sl:4 * sl + 1],
                                                                  1024.0 * EPS, None, op0=ALU.add),
                         reads=['ssn%d' % sl], writes=['ssn%d' % sl])
                    S.op('pool', lambda e, sl=sl: e.tensor_tensor(ssn[:, 4 * sl + 2:4 * sl + 3], ssn[:, 4 * sl + 1:4 * sl + 2],
                                                                  mhalf[:, 0:1], op=ALU.pow),
                         reads=['ssn%d' % sl, 'mhalf'], writes=['ssn%d' % sl])
                    S.sz = 1024
                    S.op('act', lambda e, sl=sl: e.activation(out=xn[sl][:], in_=xt[sl][:], func=AF.Copy,
                                                              scale=ssn[:, 4 * sl + 2:4 * sl + 3]),
                         reads=['xt%d' % sl, 'ssn%d' % sl], writes=['xn%d' % sl])
                    for k in range(8):
                        b = banks[k // 4]
                        dst = pvb(b).rearrange("p (k t) -> p k t", k=4)[:, k % 4, s * 128:(s + 1) * 128]
                        tp(dst, xn[sl][:, k * 128:(k + 1) * 128], ['xn%d' % sl], [pn(b)])

            def prep_b(n):
                j, T = divmod(n, NTS)
                banks = prepT.pop(n)
                for k in range(8):
                    b = banks[k // 4]
                    src = pvb(b).rearrange("p (k t) -> p k t", k=4)[:, k % 4, :]
                    S.sz = 256
                    S.op('act', lambda e, k=k, src=src: e.activation(out=uTb[n % 2][:, k, :], in_=src, func=AF.Identity,
                                                                     scale=G1[:, k, j:j + 1], bias=S1[:, k, j:j + 1]),
                         reads=[pn(b), 'G1', 'S1'], writes=['uT%d' % (n % 2)])
                ps_rel(banks[0])
                ps_rel(banks[1])

            def afm(n):
                j, T = divmod(n, NTS)
                par = n % 2
                if T == 0:
                    S.op('pool', lambda e: e.memset(conv_halo[:], 0.0), writes=['conv_halo'])
                    S.op('pool', lambda e: e.memset(pool_halo[:], 0.0), writes=['pool_halo'])
                for pair in range(2):
                    b = ps_alloc()
                    bv = pv(b).rearrange("p (a t) -> p a t", a=2)
                    for half in range(2):
                        g = 2 * pair + half
                        for k in range(8):
                            mm(bv[:, half, :], w_in[:, k, g * 128:(g + 1) * 128], uTb[n % 2][:, k, :], k == 0, k == 7,
                               [WIN[k], 'uT%d' % (n % 2)], [pn(b)])
                    sl = nxt('up', 2)
                    un = 'up%d' % sl
                    S.op('pool', lambda e, sl=sl, pair=pair: e.tensor_copy(up[sl][:, :, 0:16], pool_halo[:, 2 * pair:2 * pair + 2, :]),
                         reads=['pool_halo'], writes=[un])
                    S.sz = 512
                    S.op('dve', lambda e, sl=sl, bv=bv: e.tensor_copy(up[sl][:, :, 16:16 + TT], bv),
                         reads=[pn(b)], writes=[un])
                    ps_rel(b)
                    for half in range(2):
                        g = 2 * pair + half
                        eng = 'dve'
                        A, Bt = ta[half], tb[half]
                        An, Bn = 'ta%d' % half, 'tb%d' % half
                        u = up[sl][:, half, :]
                        L = 16 + TT
                        S.sz = 271
                        S.op(eng, lambda e, u=u, A=A: e.tensor_tensor(A[:, 1:L], u[:, 1:L], u[:, 0:L - 1], op=ALU.add),
                             reads=[un], writes=[An])
                        cur, curn = A, An
                        if g >= 1:
                            S.sz = 271
                            S.op(eng, lambda e, A=A, Bt=Bt: e.tensor_tensor(Bt[:, 3:L], A[:, 3:L], A[:, 1:L - 2], op=ALU.add),
                                 reads=[An], writes=[Bn])
                            cur, curn = Bt, Bn
                        if g >= 2:
                            S.sz = 271
                            S.op(eng, lambda e, A=A, Bt=Bt: e.tensor_tensor(A[:, 7:L], Bt[:, 7:L], Bt[:, 3:L - 4], op=ALU.add),
                                 reads=[Bn], writes=[An])
                            cur, curn = A, An
                        if g >= 3:
                            S.sz = 271
                            S.op(eng, lambda e, A=A, Bt=Bt: e.tensor_tensor(Bt[:, 15:L], A[:, 15:L], A[:, 7:L - 8], op=ALU.add),
                                 reads=[An], writes=[Bn])
                            cur, curn = Bt, Bn
                        w = POOL_W[g]
                        S.sz = 256
                        S.op('dve', lambda e, cur=cur, u=u, g=g, w=w: e.scalar_tensor_tensor(
                            out=pbf[:, g, :], in0=cur[:, 16:L], scalar=1.0 / w, in1=u[:, 16:L],
                            op0=ALU.mult, op1=ALU.subtract), reads=[curn, un], writes=['pbf%d' % g])
                        if T == 0:
                            S.op('dve', lambda e, cur=cur, g=g: e.tensor_tensor(
                                t16[:, g, :], cur[:, 16:32], vecs[:, V_INVC + g * 16:V_INVC + g * 16 + 16], op=ALU.mult),
                                 reads=[curn, 'vecs'], writes=['t16_%d' % g])
                            S.op('dve', lambda e, u=u, g=g: e.tensor_tensor(
                                pbf[:, g, 0:16], t16[:, g, :], u[:, 16:32], op=ALU.subtract),
                                 reads=['t16_%d' % g, un, 'pbf%d' % g], writes=['pbf%d' % g])
                    S.op('pool', lambda e, sl=sl, pair=pair: e.tensor_copy(pool_halo[:, 2 * pair:2 * pair + 2, :], up[sl][:, :, TT:TT + 16]),
                         reads=[un], writes=['pool_halo'])
                for pair in range(2):
                    b = ps_alloc()
                    bv = pv(b).rearrange("p (a t) -> p a t", a=2)
                    for half in range(2):
                        g = 2 * pair + half
                        mm(bv[:, half, :], w_pool[:, g, :], pbf[:, g, :], True, True, ['w_pool', 'pbf%d' % g], [pn(b)])
                    for half in range(2):
                        g = 2 * pair + half
                        S.sz = 256
                        S.op('act', lambda e, g=g, half=half, bv=bv: e.activation(
                            out=ypoolT[par][:, g, :], in_=bv[:, half, :], func=AF.Identity,
                            scale=vecs[:, V_PSC + g:V_PSC + g + 1]), reads=[pn(b), 'vecs'], writes=['ypoolT%d' % par])
                    ps_rel(b)

                for pair in range(6):
                    b = ps_alloc()
                    bv = pv(b).rearrange("p (a t) -> p a t", a=2)
                    for half in range(2):
                        e_ = 2 * pair + half
                        c0 = OFF_XBC + e_ * 128
                        for k in range(8):
                            mm(bv[:, half, :], w_in[:, k, c0:c0 + 128], uTb[n % 2][:, k, :], k == 0, k == 7,
                               [WIN[k], 'uT%d' % (n % 2)], [pn(b)])
                    sl = nxt('pre', 3)
                    prn = 'pre%d' % sl
                    S.op('pool', lambda e, sl=sl, pair=pair: e.tensor_copy(pre[sl][:, :, 0:3], conv_halo[:, 2 * pair:2 * pair + 2, :]),
                         reads=['conv_halo'], writes=[prn])
                    S.sz = 512
                    S.op('dve', lambda e, sl=sl, bv=bv: e.tensor_copy(pre[sl][:, :, 3:3 + TT], bv),
                         reads=[pn(b)], writes=[prn])
                    ps_rel(b)
                    b2 = ps_alloc()
                    bv2 = pv(b2).rearrange("p (a t) -> p a t", a=2)
                    for half in range(2):
                        e_ = 2 * pair + half
                        for kk in range(4):
                            mm(bv2[:, half, :], diagC[:, e_ * 4 + kk, :], pre[sl][:, half, kk:kk + TT], kk == 0, kk == 3,
                               ['diagC', prn], [pn(b2)])
                    for half in range(2):
                        e_ = 2 * pair + half
                        S.sz = 256
                        S.op('act', lambda e, half=half, e_=e_, bv2=bv2: e.activation(
                            out=xbcT[par][:, e_, :], in_=bv2[:, half, :], func=AF.Silu, bias=vecs[:, V_CB + e_:V_CB + e_ + 1]),
                             reads=[pn(b2), 'vecs'], writes=['xbcT%d_%d' % (par, e_)])
                    ps_rel(b2)
                    S.op('pool', lambda e, sl=sl, pair=pair: e.tensor_copy(conv_halo[:, 2 * pair:2 * pair + 2, :], pre[sl][:, :, TT:TT + 3]),
                         reads=[prn], writes=['conv_halo'])
            def atm(n):
                par = n % 2
                bdt = ps_alloc()
                dv = pv(bdt)[:, 0:32].rearrange("p (s h) -> p s h", s=2)
                for s in range(2):
                    bz = [ps_alloc(), ps_alloc()]
                    for k in range(8):
                        lt = uTb[n % 2][:, k, s * 128:(s + 1) * 128]
                        for zh in range(2):
                            mm(pv(bz[zh]), lt, w_in[:, k, OFF_Z + zh * 512:OFF_Z + (zh + 1) * 512], k == 0, k == 7,
                               [WIN[k], 'uT%d' % (n % 2)], [pn(bz[zh])])
                        mm(dv[:, s, :], lt, w_in[:, k, OFF_DT:OFF_DT + 16], k == 0, k == 7, [WIN[k], 'uT%d' % (n % 2)], [pn(bdt)])
                    for zh in range(2):
                        S.sz = 512
                        S.op('act', lambda e, s=s, zh=zh, bz=bz: e.activation(
                            out=sz[par][:, s, zh * 512:(zh + 1) * 512], in_=pv(bz[zh]), func=AF.Silu),
                             reads=[pn(bz[zh])], writes=['sz%d' % s])
                        ps_rel(bz[zh])
                S.op('dve', lambda e: e.tensor_tensor(dtb[:], dv, rowv[:, 0:16].unsqueeze(1).broadcast_to([128, 2, 16]), op=ALU.add),
                     reads=[pn(bdt), 'rowv'], writes=['dtb'])
                ps_rel(bdt)
                S.op('act', lambda e: e.activation(out=dtb[:], in_=dtb[:], func=AF.Exp), reads=['dtb'], writes=['dtb'])
                S.op('act', lambda e: e.activation(out=dtv[par][:], in_=dtb[:], func=AF.Ln, bias=1.0),
                     reads=['dtb'], writes=['dtv%d' % par])
                S.op('dve', lambda e: e.tensor_tensor(dabf[par][:], dtv[par][:], a_bc[:].unsqueeze(1).broadcast_to([128, 2, 16]), op=ALU.mult),
                     reads=['dtv%d' % par, 'a_bc'], writes=['dabf%d' % par])

            def bprep(n, c):
                par = n % 2
                cp = c
                tk = slice(c * 128, (c + 1) * 128)
                XB = ['xbcT%d_%d' % (par, e_) for e_ in range(12)]
                b = ps_alloc()
                rhs = dabf[par][:, c, :]
                for i, lt in enumerate((SLm, ones, tri)):
                    mm(pv(b)[:, 16 * i:16 * i + 16], lt, rhs, True, True, ['cst', 'dabf%d' % par], [pn(b)])
                S.op('act', lambda e, b=b: e.activation(out=sm[cp][:], in_=pv(b)[:, 0:48], func=AF.Exp),
                     reads=[pn(b)], writes=['sm%d' % cp])
                ps_rel(b)
                b = ps_alloc()
                sv = pv(b)[:, 0:256].rearrange("p (g l) -> p g l", g=2)
                for g in range(2):
                    mm(sv[:, g, :], xbcT[par][:, 8 + g, tk], xbcT[par][:, 10 + g, tk], True, True,
                       [XB[8 + g], XB[10 + g]], [pn(b)])
                S.sz = 256
                S.op('dve', lambda e, sv=sv: e.tensor_tensor(scm[cp][:], sv, tri.unsqueeze(1).broadcast_to([128, 2, 128]), op=ALU.mult),
                     reads=[pn(b), 'cst'], writes=['scm%d' % cp])
                ps_rel(b)
                for q in range(4):
                    rs = nxt('R', 4)
                    S.sz = 512
                    S.op('pool' if q % 2 == 0 else 'dve', lambda e, rs=rs, q=q: e.tensor_tensor(
                        Rr[rs][:], bc3(dabf[par][:, c, 4 * q:4 * q + 4], 128),
                        tri.unsqueeze(1).broadcast_to([128, 4, 128]), op=ALU.mult),
                         reads=['dabf%d' % par, 'cst'], writes=['Rr%d' % rs])
                    b = ps_alloc()
                    mm(pv(b), SLm, Rr[rs][:].rearrange("p a l -> p (a l)"), True, True, ['cst', 'Rr%d' % rs], [pn(b)])
                    es_ = nxt('E', 3)
                    S.sz = 512
                    S.op('act', lambda e, b=b, es_=es_: e.activation(out=Eq[es_][:].rearrange("p a l -> p (a l)"), in_=pv(b), func=AF.Exp),
                         reads=[pn(b)], writes=['Eq%d' % es_])
                    ps_rel(b)
                    S.sz = 300
                    S.op('dve', lambda e, es_=es_, q=q: e.tensor_tensor(
                        MT[cp][:, 4 * q:4 * q + 4, :], Eq[es_][:], scm[cp][:, q // 2, :].unsqueeze(1).broadcast_to([128, 4, 128]),
                        op=ALU.mult), reads=['Eq%d' % es_, 'scm%d' % cp], writes=['MT%d_%d' % (cp, q)])
                b = ps_alloc()
                for e_ in range(8):
                    tp(pvb(b)[:, e_ * 128:(e_ + 1) * 128], xbcT[par][:, e_, tk], [XB[e_]], [pn(b)])
                xv = pvb(b).rearrange("p (h q) -> p h q", q=64)
                S.sz = 1024
                S.op('dve', lambda e, xv=xv: e.tensor_tensor(xdt[cp][:].rearrange("p (h q) -> p h q", q=64), xv,
                                                            bc3(dtv[par][:, c, :], 64), op=ALU.mult),
                     reads=[pn(b), 'dtv%d' % par], writes=['xdt%d' % cp])
                ps_rel(b)
                S.sz = 1024
                S.op('dve', lambda e: e.tensor_tensor(xdtd[cp][:].rearrange("p (h q) -> p h q", q=64),
                                                       xdt[cp][:].rearrange("p (h q) -> p h q", q=64),
                                                       bc3(sm[cp][:, 0:16], 64), op=ALU.mult),
                     reads=['xdt%d' % cp, 'sm%d' % cp], writes=['xdtd%d' % cp])
                b = ps_alloc()
                for g in range(2):
                    tp(pvb(b)[:, g * 128:(g + 1) * 128], xbcT[par][:, 8 + g, tk], [XB[8 + g]], [pn(b)])
                S.sz = 256
                S.op('act', lambda e, b=b: e.activation(out=Btok[cp][:], in_=pvb(b)[:, 0:256], func=AF.Copy),
                     reads=[pn(b)], writes=['Btok%d' % cp])
                ps_rel(b)

            def bmain(n, c):
                j, T = divmod(n, NTS)
                par = n % 2
                cp = c
                tk = slice(c * 128, (c + 1) * 128)
                XB = ['xbcT%d_%d' % (par, e_) for e_ in range(12)]
                MTN = ['MT%d_%d' % (cp, q) for q in range(4)]
                if T == 0 and c == 0:
                    S.sz = 1024
                    S.op('pool', lambda e: e.memset(Hs[:], 0.0), writes=['Hs0', 'Hs1'])
                    S.sz = 512
                    S.op('pool', lambda e: e.memset(Hbf[:], 0.0), writes=['Hbf0', 'Hbf1'])
                S.op('pool', lambda e: e.memset(ssy[:, 0:2], 0.0), writes=['ssy'])
                for g in range(2):
                    gs = slice(g * 512, (g + 1) * 512)
                    byd = ps_alloc()
                    for ee in range(4):
                        e_ = 4 * g + ee
                        mm(pv(byd)[:, ee * 128:(ee + 1) * 128], xbcT[par][:, e_, tk], diagD[:, e_, :], ee == 0, True,
                           [XB[e_], 'diagD'], [pn(byd)], **({} if ee == 0 else {'skip_group_check': True}))
                    for hh in range(8):
                        h = 8 * g + hh
                        mm(pv(byd)[:, hh * 64:(hh + 1) * 64], MT[cp][:, h, :], xdt[cp][:, h * 64:(h + 1) * 64],
                           False, hh == 7, [MTN[h // 4], 'xdt%d' % cp], [pn(byd)], skip_group_check=True)
                    bst = ps_alloc()
                    mm(pv(bst), Btok[cp][:, g * 128:(g + 1) * 128], xdtd[cp][:, gs], True, True,
                       ['Btok%d' % cp, 'xdtd%d' % cp], [pn(bst)])
                    byo = ps_alloc()
                    mm(pv(byo), xbcT[par][:, 10 + g, tk], Hbf[:, gs], True, True, [XB[10 + g], 'Hbf%d' % g], [pn(byo)])
                    yg = ybuf[:, gs]
                    ygn = 'ybuf%d' % g
                    S.sz = 512
                    S.op('dve', lambda e, byo=byo, yg=yg, g=g: e.tensor_tensor(
                        yg.rearrange("p (h q) -> p h q", q=64), pv(byo).rearrange("p (h q) -> p h q", q=64),
                        bc3(sm[cp][:, 32 + 8 * g:40 + 8 * g], 64), op=ALU.mult),
                         reads=[pn(byo), 'sm%d' % cp], writes=[ygn])
                    ps_rel(byo)
                    S.sz = 512
                    S.op('dve', lambda e, byd=byd, yg=yg: e.tensor_tensor(yg, pv(byd), yg, op=ALU.add),
                         reads=[pn(byd), ygn], writes=[ygn])
                    ps_rel(byd)
                    S.sz = 512
                    S.op('dve', lambda e, yg=yg, gs=gs: e.tensor_tensor(yg, yg, sz[par][:, c, gs], op=ALU.mult),
                         reads=[ygn, 'sz%d' % c], writes=[ygn])
                    S.sz = 512
                    S.op('act', lambda e, yg=yg, gs=gs, g=g: e.activation(out=yn[:, gs], in_=yg, func=AF.Square,
                                                                          accum_out=ssy[:, g:g + 1]),
                         reads=[ygn, 'ssy'], writes=['yn%d' % g, 'ssy'])
                    Hg = Hs[:, gs]
                    S.sz = 512
                    S.op('pool', lambda e, Hg=Hg, g=g: e.tensor_tensor(
                        Hg.rearrange("p (h q) -> p h q", q=64), Hg.rearrange("p (h q) -> p h q", q=64),
                        bc3(sm[cp][:, 16 + 8 * g:24 + 8 * g], 64), op=ALU.mult),
                         reads=['Hs%d' % g, 'sm%d' % cp], writes=['Hs%d' % g])
                    S.sz = 512
                    S.op('dve', lambda e, Hg=Hg, bst=bst, gs=gs: e.tensor_tensor(Hbf[:, gs], pv(bst), Hg, op=ALU.add),
                         reads=[pn(bst), 'Hs%d' % g], writes=['Hbf%d' % g])
                    S.sz = 512
                    S.op('dve', lambda e, Hg=Hg, bst=bst: e.tensor_tensor(Hg, pv(bst), Hg, op=ALU.add),
                         reads=[pn(bst), 'Hs%d' % g], writes=['Hs%d' % g])
                    ps_rel(bst)
                S.op('pool', lambda e: e.tensor_scalar(ssy[:, 2:4], ssy[:, 0:2], 1.0 / 512.0, EPS, op0=ALU.mult, op1=ALU.add),
                     reads=['ssy'], writes=['ssy'])
                S.op('pool', lambda e: e.tensor_tensor(ssy[:, 2:4], ssy[:, 2:4], mhalf[:, 0:2], op=ALU.pow),
                     reads=['ssy', 'mhalf'], writes=['ssy'])
                for g in range(2):
                    gs = slice(g * 512, (g + 1) * 512)
                    S.sz = 512
                    S.op('act', lambda e, gs=gs, g=g: e.activation(out=yn[:, gs], in_=ybuf[:, gs], func=AF.Identity,
                                                                   scale=ssy[:, 2 + g:3 + g]),
                         reads=['ybuf%d' % g, 'ssy'], writes=['yn%d' % g])
                b = ps_alloc()
                tv = pvb(b).rearrange("p (e t) -> p e t", e=8)
                for e_ in range(8):
                    tp(tv[:, e_, :], yn[:, e_ * 128:(e_ + 1) * 128], ['yn%d' % (e_ // 4)], [pn(b)])
                S.sz = 1024
                S.op('dve', lambda e, tv=tv: e.tensor_tensor(yssdT[:, :, tk], tv, bc3(vecs[:, V_GSSD:V_GSSD + 8], 128), op=ALU.mult),
                     reads=[pn(b), 'vecs'], writes=['yssdT%d' % c])
                ps_rel(b)

            def cstage(n, s):
                j, T = divmod(n, NTS)
                par = n % 2
                tk = slice(s * 128, (s + 1) * 128)
                tok0 = T * TT + s * 128
                sl = nxt('x2', 2)
                xn_ = 'x2_%d' % sl
                S.sz = 524288
                S.dma('sp', x2[sl][:], x_d[j, tok0:tok0 + 128, :], writes=[xn_])
                bo = [ps_alloc(), ps_alloc()]
                for e_ in range(12):
                    if e_ < 4:
                        lt, ln = ypoolT[par][:, e_, tk], 'ypoolT%d' % par
                    else:
                        lt, ln = yssdT[:, e_ - 4, tk], 'yssdT%d' % s
                    for dh in range(2):
                        mm(pv(bo[dh]), lt, w_out[:, e_, dh * 512:(dh + 1) * 512], e_ == 0, e_ == 11,
                           [ln, WOUT[e_ // 4]], [pn(bo[dh])] + (['tick%d' % n] if (e_ == 0 and dh == 0 and s == 0) else []))
                for dh in range(2):
                    ds_ = slice(dh * 512, (dh + 1) * 512)
                    S.sz = 512
                    S.op('dve', lambda e, dh=dh, ds_=ds_: e.tensor_tensor(tbuf[dh][:], pv(bo[dh]), gate_m[:, j, ds_], op=ALU.mult),
                         reads=[pn(bo[dh]), 'gate_m'], writes=['tbuf%d' % dh])
                    ps_rel(bo[dh])
                    S.sz = 512
                    S.op('pool', lambda e, dh=dh, ds_=ds_, sl=sl: e.tensor_tensor(x2[sl][:, ds_], tbuf[dh][:], x2[sl][:, ds_], op=ALU.add),
                         reads=['tbuf%d' % dh, xn_], writes=[xn_])
                r0 = j * SEQ + tok0
                S.sz = 524288
                S.dma('sp', h1s_d[r0:r0 + 128, :], x2[sl][:], reads=[xn_], sem='d_h1st%d' % sl)

            def precast(i):
                S.sz = 3145728
                if i < 8:
                    S.dma('pool', wupbf_d[:, :, i, :], wup_d[i * 128:(i + 1) * 128, :].rearrange("p (b c) -> p b c", b=4),
                          sem='d_precast%d' % i, reads=['tick%d' % i], writes=['wbf%d' % i])
                else:
                    k = i - 8
                    S.dma('pool', wdnbf_d[:, 4 * k:4 * k + 4, :], wdown_d[k * 512:(k + 1) * 512, :].rearrange("(f p) d -> p f d", p=128),
                          sem='d_precast%d' % i, reads=['tick%d' % i], writes=['wbf%d' % i])
            prep_a(0)
            prep_b(0)
            afm(0)
            atm(0)
            prep_a(1)
            prep_b(1)
            for n in range(NT):
                bprep(n, 0)
                bprep(n, 1)
                if n + 1 < NT:
                    afm(n + 1)
                bmain(n, 0)
                bmain(n, 1)
                if n + 1 < NT:
                    atm(n + 1)
                if n + 2 < NT:
                    prep_a(n + 2)
                    prep_b(n + 2)
                cstage(n, 0)
                cstage(n, 1)
                precast(n)
            ps_ring[0] = None
            S.barrier()

        with ExitStack() as esF:
            w_up = sbt(esF, "w_up", [128, 4, 8, D], BF16)
            w_down = sbt(esF, "w_down", [128, 32, D], BF16)
            gate_f = sbt(esF, "gate_f", [128, 2, D])
            gfin = sbt(esF, "gfin", [128, D])
            fT = sbt(esF, "fT", [128, 32, TT], BF16)
            crepF = sbt(esF, "crepF", [128, 16, 128], BF16)
            u2T = [sbt(esF, "u2T%d" % i, [128, 8, TT], BF16) for i in range(2)]
            hs = [sbt(esF, "hs%d" % i, [128, D]) for i in range(4)]
            hn = [sbt(esF, "hn%d" % i, [128, D], BF16) for i in range(2)]
            rt = [sbt(esF, "rt%d" % i, [128, 2, TT]) for i in range(2)]
            tbf = [sbt(esF, "tbf%d" % i, [128, 512]) for i in range(2)]
            ssf = sbt(esF, "ssf", [128, 32])

            WUP = ['w_up%d' % k for k in range(4)]
            fcnt = {'hs': 0, 'rt': 0, 'hn': 0}
            fprepT = {}
            hslot = {}

            def fnxt(key, n_):
                v = fcnt[key] % n_
                fcnt[key] += 1
                return v

            def fprep(m):
                j, T = divmod(m, NTS)
                banks = [ps_alloc(), ps_alloc()]
                hslot[m] = []
                for s in range(2):
                    sl = fnxt('hs', 4)
                    hl = fnxt('hn', 2)
                    hslot[m].append(sl)
                    r0 = j * SEQ + T * TT + s * 128
                    c0 = 4 * sl
                    S.sz = 524288
                    S.dma('sp', hs[sl][:], h1s_d[r0:r0 + 128, :], writes=['hs%d' % sl])
                    S.op('pool', lambda e, c0=c0: e.memset(ssf[:, c0:c0 + 1], 0.0), writes=['ssf%d' % sl])
                    S.sz = 1024
                    S.op('act', lambda e, sl=sl, hl=hl, c0=c0: e.activation(out=hn[hl][:], in_=hs[sl][:], func=AF.Square,
                                                                            accum_out=ssf[:, c0:c0 + 1]),
                         reads=['hs%d' % sl, 'ssf%d' % sl], writes=['hn%d' % hl, 'ssf%d' % sl])
                    S.op('pool', lambda e, c0=c0: e.tensor_scalar(ssf[:, c0 + 1:c0 + 2], ssf[:, c0:c0 + 1], 1024.0 * EPS, None, op0=ALU.add),
                         reads=['ssf%d' % sl], writes=['ssf%d' % sl])
                    S.op('pool', lambda e, c0=c0: e.tensor_tensor(ssf[:, c0 + 2:c0 + 3], ssf[:, c0 + 1:c0 + 2], mhalf[:, 0:1], op=ALU.pow),
                         reads=['ssf%d' % sl, 'mhalf'], writes=['ssf%d' % sl])
                    S.sz = 1024
                    S.op('act', lambda e, sl=sl, hl=hl, c0=c0: e.activation(out=hn[hl][:], in_=hs[sl][:], func=AF.Copy, scale=ssf[:, c0 + 2:c0 + 3]),
                         reads=['hs%d' % sl, 'ssf%d' % sl], writes=['hn%d' % hl])
                    for k in range(8):
                        b = banks[k // 4]
                        dst = pvb(b).rearrange("p (k t) -> p k t", k=4)[:, k % 4, s * 128:(s + 1) * 128]
                        tp(dst, hn[hl][:, k * 128:(k + 1) * 128], ['hn%d' % hl], [pn(b)])
                up_ = m % 2
                for k in range(8):
                    b = banks[k // 4]
                    src = pvb(b).rearrange("p (k t) -> p k t", k=4)[:, k % 4, :]
                    S.sz = 256
                    S.op('act', lambda e, k=k, src=src: e.activation(out=u2T[up_][:, k, :], in_=src, func=AF.Identity,
                                                                     scale=G2[:, k, j:j + 1], bias=S2[:, k, j:j + 1]),
                         reads=[pn(b), 'G2', 'S2'], writes=['u2T%d' % up_])
                ps_rel(banks[0])
                ps_rel(banks[1])

            def fup(m):
                up_ = m % 2
                for fp in range(16):
                    b = ps_alloc()
                    bv = pv(b).rearrange("p (a t) -> p a t", a=2)
                    for half in range(2):
                        f = 2 * fp + half
                        for k in range(8):
                            mm(bv[:, half, :], w_up[:, f // 8, k, (f % 8) * 128:(f % 8 + 1) * 128], u2T[up_][:, k, :], k == 0, k == 7,
                               [WUP[f // 8], 'u2T%d' % up_], [pn(b)])
                    rl = fnxt('rt', 2)
                    S.sz = 512
                    S.op('act', lambda e, rl=rl, bv=bv: e.activation(out=rt[rl][:], in_=bv, func=AF.Relu),
                         reads=[pn(b)], writes=['rt%d' % rl])
                    ps_rel(b)
                    eng = 'pool' if fp % 3 == 2 else 'dve'
                    S.sz = 512
                    S.op(eng, lambda e, rl=rl, fp=fp: e.tensor_tensor(fT[:, 2 * fp:2 * fp + 2, :], rt[rl][:], rt[rl][:], op=ALU.mult),
                         reads=['rt%d' % rl], writes=['fT'])

            def fdown(m, s):
                j, T = divmod(m, NTS)
                tk = slice(s * 128, (s + 1) * 128)
                sl = hslot[m][s]
                hn_ = 'hs%d' % sl
                c0 = 4 * sl
                bd = [ps_alloc(), ps_alloc()]
                for f in range(32):
                    for dh in range(2):
                        mm(pv(bd[dh]), fT[:, f, tk], w_down[:, f, dh * 512:(dh + 1) * 512], f == 0, f == 31,
                           ['fT', 'w_down%d' % (f // 8)], [pn(bd[dh])])
                for dh in range(2):
                    ds_ = slice(dh * 512, (dh + 1) * 512)
                    S.sz = 512
                    S.op('dve', lambda e, dh=dh, ds_=ds_: e.tensor_tensor(tbf[dh][:], pv(bd[dh]), gate_f[:, j, ds_], op=ALU.mult),
                         reads=[pn(bd[dh]), 'gate_f'], writes=['tbf%d' % dh])
                    ps_rel(bd[dh])
                    S.sz = 512
                    S.op('pool', lambda e, dh=dh, ds_=ds_: e.tensor_tensor(hs[sl][:, ds_], tbf[dh][:], hs[sl][:, ds_], op=ALU.add),
                         reads=['tbf%d' % dh, hn_], writes=[hn_])
                hl = fnxt('hn', 2)
                S.op('pool', lambda e: e.memset(ssf[:, c0:c0 + 1], 0.0), writes=['ssf%d' % sl])
                S.sz = 1024
                S.op('act', lambda e: e.activation(out=hn[hl][:], in_=hs[sl][:], func=AF.Square, accum_out=ssf[:, c0:c0 + 1]),
                     reads=[hn_, 'ssf%d' % sl], writes=['hn%d' % hl, 'ssf%d' % sl])
                S.op('pool', lambda e: e.tensor_scalar(ssf[:, c0 + 1:c0 + 2], ssf[:, c0:c0 + 1], 1.0 / 1024.0, EPS, op0=ALU.mult, op1=ALU.add),
                     reads=['ssf%d' % sl], writes=['ssf%d' % sl])
                S.op('pool', lambda e: e.tensor_tensor(ssf[:, c0 + 2:c0 + 3], ssf[:, c0 + 1:c0 + 2], mhalf[:, 0:1], op=ALU.pow),
                     reads=['ssf%d' % sl, 'mhalf'], writes=['ssf%d' % sl])
                S.sz = 1024
                S.op('dve', lambda e: e.scalar_tensor_tensor(out=hs[sl][:], in0=hs[sl][:], scalar=ssf[:, c0 + 2:c0 + 3], in1=gfin[:],
                                                             op0=ALU.mult, op1=ALU.mult),
                     reads=[hn_, 'ssf%d' % sl, 'gfin'], writes=[hn_])
                tok0 = T * TT + s * 128
                S.sz = 524288
                S.dma('sp', out_d[j, tok0:tok0 + 128, :], hs[sl][:], reads=[hn_], sem='d_out%d' % sl)

            fprep(0)
            for fb in range(4):
                S.sz = 2097152
                S.dma('sp', w_up[:, fb, :, :], wupbf_d[:, fb, :, :], writes=['w_up%d' % fb])
            make_crep(crepF)
            for j in range(2):
                S.sz = 524288
                S.dma('sp', gate_f[:, j, :], gbias_d[:, 1024:2048], writes=['gate_f'], sem='d_gf%d' % j)
            S.sz = 524288
            S.dma('sp', gfin[:], gfin_d, writes=['gfin'])
            wadaF = fT[:].rearrange("p f t -> p (f t)").rearrange("p (k c) -> p k c", k=8)
            ada_block(5, wadaF, 'fT', crepF, gate=(gate_f, 'gate_f'))
            for i in range(4):
                S.sz = 2097152
                S.dma('sp', w_down[:, 8 * i:8 * i + 8, :], wdnbf_d[:, 8 * i:8 * i + 8, :], writes=['w_down%d' % i])
            for m in range(NT):
                fup(m)
                if m + 1 < NT:
                    fprep(m + 1)
                fdown(m, 0)
                fdown(m, 1)
            S.barrier()
    return nc


_NC_CACHE = {}


def _consts():
    i = np.arange(128)
    ident = np.eye(128, dtype=np.float32)
    tri = (i[:, None] <= i[None, :]).astype(np.float32)
    sl = (i[:, None] > i[None, :]).astype(np.float32)
    ones = np.ones((128, 128), np.float32)
    return np.ascontiguousarray(np.concatenate([ident, tri, sl, ones], axis=1))


def _fm(v, nchunk):
    return np.ascontiguousarray(np.asarray(v, np.float32).reshape(nchunk, 128).T)


def make_in_maps(x, c, w_ada, b_ada, g_mix, w_in, conv_w, conv_b, dt_bias, a_log, d_skip, g_ssd,
                 w_pool, pool_scale, w_out, g_mlp, w_up, w_down, g_final):
    f = lambda a: np.ascontiguousarray(np.asarray(a, dtype=np.float32))
    x = f(x); c = f(c)
    w_ada0 = f(w_ada[0]); b0 = f(b_ada[0])
    consts = _consts()
    invc = np.zeros((128, 4, 16), np.float32)
    for g, w in enumerate(POOL_W):
        invc[:, g, :] = 1.0 / np.minimum(np.arange(1, 17), w)
    cw = f(conv_w[0])
    cw_fm = np.ascontiguousarray(cw.T.reshape(12, 128, 4).transpose(1, 0, 2)).reshape(128, 48)
    gbias = np.ascontiguousarray(np.broadcast_to(np.concatenate([b0[2048:3072], b0[5120:6144]])[None, :], (128, 2048)))
    gfin = np.ascontiguousarray(np.broadcast_to(f(g_final)[None, :], (128, D)))
    rowv = np.ascontiguousarray(np.broadcast_to(
        np.concatenate([f(dt_bias[0]), f(a_log[0]), f(d_skip[0])])[None, :], (128, 48)))
    shared = dict(consts=consts, gbias=gbias, gfin=gfin, rowv=rowv, w_ada=w_ada0, w_in=f(w_in[0]),
                  w_out=f(w_out[0]), w_up=f(w_up[0]), w_down=f(w_down[0]),
                  w_pool=f(w_pool[0]).reshape(512, 128))
    in_maps = []
    for core in range(NCORES):
        vecs = np.zeros((128, NV), np.float32)
        vecs[:, V_GMIX:V_GMIX + 8] = _fm(g_mix[0], 8)
        vecs[:, V_GMLP:V_GMLP + 8] = _fm(g_mlp[0], 8)
        vecs[:, V_CW:V_CW + 48] = cw_fm
        vecs[:, V_CB:V_CB + 12] = _fm(conv_b[0], 12)
        vecs[:, V_PSC:V_PSC + 4] = _fm(pool_scale[0], 4)
        vecs[:, V_GSSD:V_GSSD + 8] = _fm(g_ssd[0], 8)
        vecs[:, V_BT:V_BT + 48] = _fm(b0, 48)
        vecs[:, V_INVC:V_INVC + 64] = invc.reshape(128, 64)
        vecs[:, V_DSK:V_DSK + 8] = _fm(np.repeat(f(d_skip[0]), 64), 8)
        cc = c[NB * core:NB * core + NB]
        vecs[:, V_CT:V_CT + 16] = np.ascontiguousarray(cc.reshape(NB, 8, 128).transpose(2, 1, 0)).reshape(128, 16)
        m = dict(shared)
        m["x"] = np.ascontiguousarray(x[NB * core:NB * core + NB])
        m["vecs"] = vecs
        in_maps.append(m)
    return in_maps


def kernel(**inputs):
    if "nc" not in _NC_CACHE:
        _NC_CACHE["nc"] = build_nc()
    nc = _NC_CACHE["nc"]
    in_maps = make_in_maps(**inputs)
    res = run_bass_kernel_spmd(nc, in_maps, core_ids=list(range(NCORES)))
    out = np.concatenate([np.asarray(r["out"]) for r in res.results], axis=0)
    return out.astype(np.float32)
```
